# Optimizing a Trainium2 kernel written in Bass

```python
import math
import jax, jax.numpy as jnp
from jax import lax
import numpy as np

D_MODEL = 1024
BATCH = 8
SEQ = 8192
DEPTH = 2
DEC_BATCH = 2
DEC_SEQ = 8192
PAST_LEN = 128

ATTN_WIDTH = 512
LRU_WIDTH = 512
N_DIFF_HEADS = 4
QK_HEAD_DIM = 64
V_HEAD_DIM = 2 * QK_HEAD_DIM
ROT_DIM = QK_HEAD_DIM // 4
ROPE_THETA = 500000.0
Q_BLOCK = 128
CONV_WIDTH = 4
CONV_LEFT = 2
LRU_BLOCKS = 8
LRU_BLOCK_DIM = LRU_WIDTH // LRU_BLOCKS
LRU_C = 8.0
PEER_HEADS = 8
N_KEYS = 128
N_EXPERTS = N_KEYS * N_KEYS
PEER_TOPK = 16
PEER_HALF = 128
PEER_QUERY_DIM = 2 * PEER_HALF
PEER_CHUNK = 128
IN_WIDTH = 3 * ATTN_WIDTH + 2 * LRU_WIDTH
DN_ALPHA = (2 * DEPTH) ** 0.25
DN_BETA = (8 * DEPTH) ** -0.25
LN_EPS = 1e-5

kernel_name = "hymba_diffattn_rglru_peer_encoder"

F32 = jnp.float32


def layer_norm(x, g, b):
    xf = x.astype(F32)
    mu = jnp.mean(xf, axis=-1, keepdims=True)
    var = jnp.mean(jnp.square(xf - mu), axis=-1, keepdims=True)
    return (xf - mu) * lax.rsqrt(var + LN_EPS) * g.astype(F32) + b.astype(F32)


def rotary_tables(seq):
    inv = ROPE_THETA ** (-jnp.arange(0, ROT_DIM, 2, dtype=F32) / ROT_DIM)
    ang = jnp.arange(seq, dtype=F32)[:, None] * inv[None, :]
    return jnp.cos(ang), jnp.sin(ang)


def partial_rotary(t, cos, sin):
    half = ROT_DIM // 2
    r1 = t[..., :half]
    r2 = t[..., half:ROT_DIM]
    c = cos[None, :, None, None, :]
    s = sin[None, :, None, None, :]
    return jnp.concatenate([r1 * c - r2 * s, r2 * c + r1 * s, t[..., ROT_DIM:]], axis=-1)


def diff_attention(q, k, v, lam, subln_g, lambda_init):
    B, S = q.shape[0], q.shape[1]
    nq = S // Q_BLOCK
    qb = jnp.moveaxis(q.reshape(B, nq, Q_BLOCK, N_DIFF_HEADS, 2, QK_HEAD_DIM), 1, 0)
    scale = QK_HEAD_DIM ** -0.5
    kf = k.astype(F32)
    vf = v.astype(F32)

    def block(qc):
        s = jnp.einsum('bqhcd,bkhcd->bhcqk', qc.astype(F32), kf) * scale
        p = jax.nn.softmax(s, axis=-1)
        a = p[:, :, 0] - lam * p[:, :, 1]
        return jnp.einsum('bhqk,bkhe->bqhe', a, vf)

    o = lax.map(block, qb)
    o = jnp.moveaxis(o, 0, 1).reshape(B, S, N_DIFF_HEADS, V_HEAD_DIM)
    o = o * lax.rsqrt(jnp.mean(o * o, axis=-1, keepdims=True) + 1e-5) * subln_g.astype(F32)
    return (o * (1.0 - lambda_init)).reshape(B, S, ATTN_WIDTH)


def centred_dwconv(x, w, b):
    S = x.shape[1]
    xp = jnp.pad(x, ((0, 0), (CONV_LEFT, CONV_WIDTH - 1 - CONV_LEFT), (0, 0)))
    out = xp[:, 0:S] * w[0]
    for j in range(1, CONV_WIDTH):
        out = out + xp[:, j:j + S] * w[j]
    return out + b


def rg_lru(x, w_a, b_a, w_i, b_i, lam, reverse):
    B, S, W = x.shape
    xb = x.reshape(B, S, LRU_BLOCKS, LRU_BLOCK_DIM)
    r = jax.nn.sigmoid(jnp.einsum('bsnd,nde->bsne', xb, w_a).reshape(B, S, W) + b_a)
    i = jax.nn.sigmoid(jnp.einsum('bsnd,nde->bsne', xb, w_i).reshape(B, S, W) + b_i)
    log_a = (-LRU_C * r * jax.nn.softplus(-lam.astype(F32))).astype(F32)
    a = jnp.exp(log_a)
    bt = (jnp.sqrt(-jnp.expm1(2.0 * log_a)) * (i * x)).astype(F32)

    def combine(lhs, rhs):
        a1, b1 = lhs
        a2, b2 = rhs
        return a1 * a2, a2 * b1 + b2

    _, h = lax.associative_scan(combine, (a, bt), axis=1, reverse=reverse)
    return h


def recurrent_mixer(xr, gate, conv_w, conv_b, ga_w, ga_b, gi_w, gi_b, lru_lam):
    xc = centred_dwconv(xr.astype(F32), conv_w.astype(F32), conv_b.astype(F32))
    h = (rg_lru(xc, ga_w[0], ga_b[0], gi_w[0], gi_b[0], lru_lam[0], False)
         + rg_lru(xc, ga_w[1], ga_b[1], gi_w[1], gi_b[1], lru_lam[1], True))
    return h * jax.nn.gelu(gate.astype(F32))


def peer(x, wq, sub_keys, eu, ev):
    B, S, D = x.shape
    xt = x.reshape(-1, PEER_CHUNK, D)

    def chunk(xc):
        xc = xc.astype(F32)
        q = (xc @ wq).reshape(PEER_CHUNK, PEER_HEADS, 2, PEER_HALF)
        s = jnp.einsum('chpd,pkd->chpk', q, sub_keys)
        s_top, i_top = lax.top_k(s, PEER_TOPK)
        cand = (s_top[:, :, 0, :, None] + s_top[:, :, 1, None, :]).reshape(
            PEER_CHUNK, PEER_HEADS, PEER_TOPK * PEER_TOPK)
        f_s, f_i = lax.top_k(cand, PEER_TOPK)
        e1 = jnp.take_along_axis(i_top[:, :, 0], f_i // PEER_TOPK, axis=-1)
        e2 = jnp.take_along_axis(i_top[:, :, 1], f_i % PEER_TOPK, axis=-1)
        idx = (e1 * N_KEYS + e2).reshape(PEER_CHUNK, PEER_HEADS * PEER_TOPK)
        g = jax.nn.softmax(f_s.astype(F32), axis=-1).reshape(PEER_CHUNK, PEER_HEADS * PEER_TOPK)
        u = jnp.take(eu, idx, axis=0).astype(F32)
        vv = jnp.take(ev, idx, axis=0).astype(F32)
        act = jax.nn.gelu(jnp.einsum('cnd,cd->cn', u, xc))
        return jnp.einsum('cn,cnd->cd', g * act, vv)

    return lax.map(chunk, xt).reshape(B, S, D)


def encoder_layer(x, l, w_in, lambda_qk, subln_g, conv_w, conv_b, gate_a_w, gate_a_b,
                  gate_i_w, gate_i_b, lru_lambda, w_out, ln1_g, ln1_b, peer_wq, peer_keys,
                  expert_u, expert_v, ln2_g, ln2_b):
    dtype = x.dtype
    B, S = x.shape[0], x.shape[1]
    lambda_init = 0.8 - 0.6 * math.exp(-0.3 * l)
    proj = x @ w_in[l]
    q, k, v, xr, gate = jnp.split(
        proj, [ATTN_WIDTH, 2 * ATTN_WIDTH, 3 * ATTN_WIDTH, 3 * ATTN_WIDTH + LRU_WIDTH], axis=-1)
    q = q.reshape(B, S, N_DIFF_HEADS, 2, QK_HEAD_DIM).astype(F32)
    k = k.reshape(B, S, N_DIFF_HEADS, 2, QK_HEAD_DIM).astype(F32)
    v = v.reshape(B, S, N_DIFF_HEADS, V_HEAD_DIM)
    cos, sin = rotary_tables(S)
    q = partial_rotary(q, cos, sin)
    k = partial_rotary(k, cos, sin)
    lp = lambda_qk[l].astype(F32)
    lam = jnp.exp(jnp.sum(lp[0] * lp[1])) - jnp.exp(jnp.sum(lp[2] * lp[3])) + lambda_init
    attn = diff_attention(q, k, v, lam, subln_g[l], lambda_init)
    rec = recurrent_mixer(xr, gate, conv_w[l], conv_b[l], gate_a_w[l], gate_a_b[l],
                          gate_i_w[l], gate_i_b[l], lru_lambda[l])
    mix = jnp.concatenate([attn, rec], axis=-1) @ w_out[l].astype(F32)
    x = layer_norm(DN_ALPHA * x.astype(F32) + mix, ln1_g[l], ln1_b[l]).astype(dtype)
    ffn = peer(x, peer_wq[l], peer_keys[l], expert_u[l], expert_v[l])
    x = layer_norm(DN_ALPHA * x.astype(F32) + ffn, ln2_g[l], ln2_b[l]).astype(dtype)
    return x


def trunk(x, w_in, lambda_qk, subln_g, conv_w, conv_b, gate_a_w, gate_a_b, gate_i_w,
          gate_i_b, lru_lambda, w_out, ln1_g, ln1_b, peer_wq, peer_keys, expert_u,
          expert_v, ln2_g, ln2_b):
    for l in range(DEPTH):
        x = encoder_layer(x, l, w_in, lambda_qk, subln_g, conv_w, conv_b, gate_a_w, gate_a_b,
                          gate_i_w, gate_i_b, lru_lambda, w_out, ln1_g, ln1_b, peer_wq,
                          peer_keys, expert_u, expert_v, ln2_g, ln2_b)
    return x


def setup_inputs(seed: int = 0) -> dict:
    key = jax.random.key(seed)
    ks = jax.random.split(key, 24)
    nrm = lambda k, shape, s: jax.random.normal(k, shape, F32) * s
    w_in = nrm(ks[2], (DEPTH, D_MODEL, IN_WIDTH), D_MODEL ** -0.5)
    w_in = w_in.at[..., 2 * ATTN_WIDTH:3 * ATTN_WIDTH].multiply(DN_BETA)
    u_lru = jax.random.uniform(ks[11], (DEPTH, 2, LRU_WIDTH), F32, 0.9, 0.999)
    return {
        "x_prompt": nrm(ks[0], (BATCH, SEQ, D_MODEL), 1.0),
        "x_sample": nrm(ks[1], (DEC_BATCH, DEC_SEQ, D_MODEL), 1.0),
        "w_in": w_in,
        "lambda_qk": nrm(ks[3], (DEPTH, 4, QK_HEAD_DIM), 0.1),
        "subln_g": 1.0 + nrm(ks[4], (DEPTH, V_HEAD_DIM), 0.02),
        "conv_w": nrm(ks[5], (DEPTH, CONV_WIDTH, LRU_WIDTH), CONV_WIDTH ** -0.5),
        "conv_b": nrm(ks[6], (DEPTH, LRU_WIDTH), 0.01),
        "gate_a_w": nrm(ks[7], (DEPTH, 2, LRU_BLOCKS, LRU_BLOCK_DIM, LRU_BLOCK_DIM), LRU_BLOCK_DIM ** -0.5),
        "gate_a_b": nrm(ks[8], (DEPTH, 2, LRU_WIDTH), 0.01),
        "gate_i_w": nrm(ks[9], (DEPTH, 2, LRU_BLOCKS, LRU_BLOCK_DIM, LRU_BLOCK_DIM), LRU_BLOCK_DIM ** -0.5),
        "gate_i_b": nrm(ks[10], (DEPTH, 2, LRU_WIDTH), 0.01),
        "lru_lambda": jnp.log(u_lru) - jnp.log1p(-u_lru),
        "w_out": nrm(ks[12], (DEPTH, ATTN_WIDTH + LRU_WIDTH, D_MODEL), DN_BETA * (ATTN_WIDTH + LRU_WIDTH) ** -0.5),
        "ln1_g": 1.0 + nrm(ks[13], (DEPTH, D_MODEL), 0.02),
        "ln1_b": nrm(ks[14], (DEPTH, D_MODEL), 0.01),
        "peer_wq": nrm(ks[15], (DEPTH, D_MODEL, PEER_HEADS * PEER_QUERY_DIM), D_MODEL ** -0.5),
        "peer_keys": nrm(ks[16], (DEPTH, 2, N_KEYS, PEER_HALF), PEER_HALF ** -0.5),
        "expert_u": nrm(ks[17], (DEPTH, N_EXPERTS, D_MODEL), D_MODEL ** -0.5),
        "expert_v": nrm(ks[18], (DEPTH, N_EXPERTS, D_MODEL), DN_BETA * PEER_HEADS ** -0.5),
        "ln2_g": 1.0 + nrm(ks[19], (DEPTH, D_MODEL), 0.02),
        "ln2_b": nrm(ks[20], (DEPTH, D_MODEL), 0.01),
    }


def reference(x_prompt, x_sample, w_in, lambda_qk, subln_g, conv_w, conv_b, gate_a_w, gate_a_b,
              gate_i_w, gate_i_b, lru_lambda, w_out, ln1_g, ln1_b, peer_wq, peer_keys,
              expert_u, expert_v, ln2_g, ln2_b):
    y_prompt = trunk(x_prompt, w_in, lambda_qk, subln_g, conv_w, conv_b, gate_a_w, gate_a_b,
                     gate_i_w, gate_i_b, lru_lambda, w_out, ln1_g, ln1_b, peer_wq, peer_keys,
                     expert_u, expert_v, ln2_g, ln2_b)
    y_sample = trunk(x_sample, w_in, lambda_qk, subln_g, conv_w, conv_b, gate_a_w, gate_a_b,
                     gate_i_w, gate_i_b, lru_lambda, w_out, ln1_g, ln1_b, peer_wq, peer_keys,
                     expert_u, expert_v, ln2_g, ln2_b)
    return (y_prompt, y_sample)
```

```python
import math
from contextlib import ExitStack

import numpy as np
import concourse.bass as bass
import concourse.mybir as mybir
from concourse.bass_utils import run_bass_kernel_spmd

F32 = mybir.dt.float32
BF16 = mybir.dt.bfloat16
U32 = mybir.dt.uint32
AF = mybir.ActivationFunctionType
ALU = mybir.AluOpType
AX = mybir.AxisListType

D = 1024
INW = 2560
NE = 16384
ALPHA = 4.0 ** 0.25
LN_EPS = 1e-5
NEG = -1.0e30


class SemRef:
    __slots__ = ("h", "count", "name")

    def __init__(self, h, name):
        self.h = h
        self.count = 0
        self.name = name


class Buf:
    __slots__ = ("name", "w", "rs", "dsem")

    def __init__(self, name=""):
        self.name = name
        self.w = None
        self.rs = {}
        self.dsem = None


class Eng:
    def __init__(self, ctx, name, eng, self_sync=True):
        self.ctx = ctx
        self.name = name
        self.eng = eng
        self.sem = ctx.new_sem("e_" + name)
        self.seen = {}
        self.self_sync = self_sync

    def _waits(self, reads, writes, extra=()):
        need = {}
        for b in reads:
            if b.w is not None:
                k, v = b.w
                if need.get(k, 0) < v:
                    need[k] = v
        for b in writes:
            if b.w is not None:
                k, v = b.w
                if need.get(k, 0) < v:
                    need[k] = v
            for k, v in b.rs.items():
                if need.get(k, 0) < v:
                    need[k] = v
        for k, v in extra:
            if need.get(k, 0) < v:
                need[k] = v
        for k, v in need.items():
            if k is self.sem and not self.self_sync:
                continue
            if self.seen.get(k, 0) >= v:
                continue
            self.eng.wait_ge(k.h, v)
            self.seen[k] = v

    def op(self, fn, reads=(), writes=(), inc=True):
        self._waits(reads, writes)
        ins = fn(self.eng)
        tok = (self.sem, self.sem.count + 1)
        if inc:
            ins.then_inc(self.sem.h, 1)
            self.sem.count += 1
        for b in reads:
            if b.rs.get(tok[0], 0) < tok[1]:
                b.rs[tok[0]] = tok[1]
        for b in writes:
            b.w = tok
            b.rs = {}
        return ins

    def dma(self, out, in_, reads, writes, sembuf=None, **kw):
        sb = sembuf if sembuf is not None else (writes[0] if writes else reads[0])
        if sb.dsem is None:
            sb.dsem = self.ctx.get_dsem()
        ds = sb.dsem
        extra = [(ds, ds.count)] if ds.count > 0 else []
        self._waits(reads, writes, extra)
        ins = self.eng.dma_start(out=out, in_=in_, **kw)
        ins.then_inc(ds.h, 16)
        ds.count += 16
        tok = (ds, ds.count)
        for b in reads:
            if b.rs.get(tok[0], 0) < tok[1]:
                b.rs[tok[0]] = tok[1]
        for b in writes:
            b.w = tok
            b.rs = {}
        return ins


class Ctx:
    def __init__(self, nc):
        self.nc = nc
        self.es = ExitStack()
        self.nsem = 0
        self.allsems = []
        self.free_dsems = []
        self.phase_dsems = []
        self.uid = 0
        self.pe = Eng(self, "pe", nc.tensor, self_sync=False)
        self.act = Eng(self, "act", nc.scalar)
        self.dve = Eng(self, "dve", nc.vector)
        self.pool = Eng(self, "pool", nc.gpsimd)
        self.sp = Eng(self, "sp", nc.sync)
        self.engs = [self.pe, self.act, self.dve, self.pool, self.sp]

    def new_sem(self, name):
        self.nsem += 1
        h = self.es.enter_context(self.nc.semaphore("%s_%d" % (name, self.nsem)))
        s = SemRef(h, name)
        self.allsems.append(s)
        return s

    def get_dsem(self):
        if self.free_dsems:
            s = self.free_dsems.pop()
        else:
            s = self.new_sem("d")
        self.phase_dsems.append(s)
        return s

    def barrier(self):
        for e in self.engs:
            for s in self.allsems:
                if s.count > 0 and e.seen.get(s, 0) < s.count and not (s is e.sem):
                    e.eng.wait_ge(s.h, s.count)
                    e.seen[s] = s.count
        self.free_dsems.extend(self.phase_dsems)
        self.phase_dsems = []

    def sb(self, es, name, shape, dtype):
        self.uid += 1
        t = es.enter_context(self.nc.sbuf_tensor("%s_%d" % (name, self.uid), list(shape), dtype))
        return t, Buf(name)

    def ring(self, es, name, shape, dtype, n):
        return Ring([self.sb(es, name, shape, dtype) for _ in range(n)])


class Ring:
    def __init__(self, items):
        self.items = items
        self.i = 0

    def next(self):
        it = self.items[self.i % len(self.items)]
        self.i += 1
        return it


def build_nc(S=8192, NSLOT=2, DEPTH=2, dbg=False):
    nc = bass.Bass("TRN2", target_bir_lowering=False)
    NTB = S // 512
    NKT = S // 128
    SBW = min(S, 2048)
    NSB = S // SBW
    TG = 256
    NG = S // TG

    def din(name, shape, dt=F32):
        return nc.dram_tensor(name, list(shape), dt, kind="ExternalInput").ap()

    def dscr(name, shape, dt):
        kind = "ExternalOutput" if dbg else "Internal"
        return nc.dram_tensor(name, list(shape), dt, kind=kind).ap()

    x_in = din("x", [NSLOT, S, D])
    w_in = din("w_in", [DEPTH, D, INW])
    lambda_qk = din("lambda_qk", [DEPTH, 256])
    subln_g = din("subln_g", [DEPTH, 128])
    conv_w = din("conv_w", [DEPTH, 4, 512])
    conv_b = din("conv_b", [DEPTH, 512])
    gate_a_w = din("gate_a_w", [DEPTH, 2, 8, 64, 64])
    gate_a_b = din("gate_a_b", [DEPTH, 2, 512])
    gate_i_w = din("gate_i_w", [DEPTH, 2, 8, 64, 64])
    gate_i_b = din("gate_i_b", [DEPTH, 2, 512])
    lru_lambda = din("lru_lambda", [DEPTH, 2, 512])
    w_out = din("w_out", [DEPTH, D, D])
    ln1_g = din("ln1_g", [DEPTH, D])
    ln1_b = din("ln1_b", [DEPTH, D])
    peer_wq = din("peer_wq", [DEPTH, D, 2048])
    peer_keys = din("peer_keys", [DEPTH, 2, 128, 128])
    expert_u = din("expert_u", [DEPTH, NE, D])
    expert_v = din("expert_v", [DEPTH, NE, D])
    ln2_g = din("ln2_g", [DEPTH, D])
    ln2_b = din("ln2_b", [DEPTH, D])
    ident_d = din("c_ident", [128, 128])
    rotc_d = din("c_rotc", [128, S])
    rots_d = din("c_rots", [128, S])
    iota_d = din("c_iota", [128, 128])

    y_out = nc.dram_tensor("y", [NSLOT, S, D], F32, kind="ExternalOutput").ap()

    xmid = dscr("s_xmid", [S, D], F32)
    x1_s = dscr("s_x1", [S, D], F32)
    qT_s = dscr("s_qT", [4, 128, S], BF16)
    kT_s = dscr("s_kT", [4, 128, S], BF16)
    v_s = dscr("s_v", [S, 512], BF16)
    xg_s = dscr("s_xg", [8, 128, S], F32)
    cat_s = dscr("s_cat", [8, 128, S], BF16)
    euT_s = dscr("s_euT", [DEPTH, 128, 128, 1024], BF16)
    ev_s = dscr("s_ev", [DEPTH, 128, 128, 1024], BF16)
    b_xmid, b_x1s, b_qT, b_kT, b_v, b_xg, b_cat = (Buf(n) for n in
                                                    ("xmid", "x1s", "qT", "kT", "v", "xg", "cat"))
    b_eu, b_ev, b_y = Buf("euT"), Buf("evs"), Buf("y")

    c = Ctx(nc)
    pe, act, dve, pool, sp = c.pe, c.act, c.dve, c.pool, c.sp

    with c.es, ExitStack() as ges:
        banks = []
        for i in range(8):
            t = ges.enter_context(nc.psum_tensor("bank%d" % i, [128, 512], F32))
            banks.append((t, Buf("bank%d" % i)))
        ident, b_ident = c.sb(ges, "ident", [128, 128], F32)
        iota, b_iota = c.sb(ges, "iota", [128, 128], F32)
        ones_bf, b_ones_bf = c.sb(ges, "ones_bf", [128, 128], BF16)
        ones_f, b_ones_f = c.sb(ges, "ones_f", [128, 128], F32)
        sp.dma(ident[:], ident_d[:, :], [], [b_ident])
        sp.dma(iota[:], iota_d[:, :], [], [b_iota])
        dve.op(lambda e: e.memset(ones_bf[:], 1.0), [], [b_ones_bf])
        dve.op(lambda e: e.memset(ones_f[:], 1.0), [], [b_ones_f])

        alt = [0]

        def evac(out_ap, in_ap, reads, writes):
            alt[0] += 1
            if alt[0] % 2:
                dve.op(lambda e: e.tensor_copy(out_ap, in_ap), reads, writes)
            else:
                act.op(lambda e: e.copy(out_ap, in_ap), reads, writes)

        def phase_tables():
            with ExitStack() as es:
                ld = c.ring(es, "t_ld", [128, 1024], F32, 4)
                euo = c.ring(es, "t_euo", [128, 8, 128], BF16, 2)
                evo = c.ring(es, "t_evo", [128, 1024], BF16, 2)
                for l in range(DEPTH):
                    for k1 in range(128):
                        ut, b_ut = ld.next()
                        sp.dma(ut[:], expert_u[l, k1 * 128:(k1 + 1) * 128, :], [], [b_ut])
                        vt, b_vt = ld.next()
                        sp.dma(vt[:], expert_v[l, k1 * 128:(k1 + 1) * 128, :], [], [b_vt])
                        eo, b_eo = euo.next()
                        for half in range(2):
                            bk, b_bk = banks[(k1 * 2 + half) % 4]
                            for i in range(4):
                                cc = half * 4 + i
                                pe.op(lambda e: e.transpose(bk[:, i * 128:(i + 1) * 128],
                                                            ut[:, cc * 128:(cc + 1) * 128], ident[:]),
                                      [b_ut, b_ident], [b_bk], inc=(i == 3))
                            evac(eo[:, half * 4:(half + 1) * 4, :],
                                 bk[:].rearrange("p (i k) -> p i k", i=4), [b_bk], [b_eo])
                        pool.dma(euT_s[l, k1].rearrange("p (c k) -> p c k", c=8), eo[:],
                                 [b_eo], [b_eu], sembuf=b_eo)
                        vo, b_vo = evo.next()
                        if k1 % 2:
                            act.op(lambda e: e.copy(vo[:], vt[:]), [b_vt], [b_vo])
                        else:
                            pool.op(lambda e: e.tensor_copy(vo[:], vt[:]), [b_vt], [b_vo])
                        pool.dma(ev_s[l, k1], vo[:], [b_vo], [b_ev], sembuf=b_vo)
            c.barrier()

        def phase_proj(l, xsrc, b_xsrc):
            with ExitStack() as es:
                wbf, b_wbf = c.sb(es, "wbf", [128, 8, INW], BF16)
                wrot, b_wrot = c.sb(es, "wrot", [128, 8, 1024], BF16)
                with ExitStack() as es2:
                    wst = c.ring(es2, "wst", [128, INW], F32, 2)
                    for cc in range(8):
                        st, b_st = wst.next()
                        sp.dma(st[:], w_in[l, cc * 128:(cc + 1) * 128, :], [], [b_st])
                        if cc % 2:
                            act.op(lambda e: e.copy(wbf[:, cc, :], st[:]), [b_st], [b_wbf])
                        else:
                            dve.op(lambda e: e.tensor_copy(wbf[:, cc, :], st[:]), [b_st], [b_wbf])
                    pool.op(lambda e: e.memset(wrot[:], 0.0), [], [b_wrot])
                    for cc in range(8):
                        src = wbf[:, cc, 0:1024].rearrange("p (b j) -> p b j", j=64)
                        dst = wrot[:, cc, :].rearrange("p (b j) -> p b j", j=64)
                        dve.op(lambda e: e.tensor_scalar(dst[:, :, 0:8], src[:, :, 8:16], -1.0, None,
                                                         ALU.mult), [b_wbf], [b_wrot])
                        dve.op(lambda e: e.tensor_copy(dst[:, :, 8:16], src[:, :, 0:8]),
                               [b_wbf], [b_wrot])
                    c.barrier()
                xring = c.ring(es, "p_x", [128, 4, D], F32, 2)
                xTring = c.ring(es, "p_xT", [128, 8, 512], BF16, 2)
                rcring = c.ring(es, "p_rc", [128, 512], F32, 2)
                rsring = c.ring(es, "p_rs", [128, 512], F32, 2)
                t1ring = c.ring(es, "p_t1", [128, 512], F32, 2)
                t2ring = c.ring(es, "p_t2", [128, 512], F32, 2)
                qoring = c.ring(es, "p_qo", [128, 512], BF16, 3)
                voring = c.ring(es, "p_vo", [128, 4, 512], BF16, 2)
                xgring = c.ring(es, "p_xg", [128, 512], F32, 3)
                bi = [0]

                def nbank():
                    bi[0] += 1
                    return banks[bi[0] % 8]

                for tb in range(NTB):
                    tsl = slice(tb * 512, (tb + 1) * 512)
                    xt, b_xt = xring.next()
                    sp.dma(xt[:], xsrc[tsl, :].rearrange("(j p) d -> p j d", p=128), [b_xsrc], [b_xt])
                    rc, b_rc = rcring.next()
                    sp.dma(rc[:], rotc_d[:, tsl], [], [b_rc])
                    rs, b_rs = rsring.next()
                    sp.dma(rs[:], rots_d[:, tsl], [], [b_rs])
                    xT, b_xT = xTring.next()
                    for j in range(4):
                        for half in range(2):
                            bk, b_bk = nbank()
                            for i in range(4):
                                cc = half * 4 + i
                                pe.op(lambda e: e.transpose(bk[:, i * 128:(i + 1) * 128],
                                                            xt[:, j, cc * 128:(cc + 1) * 128], ident[:]),
                                      [b_xt, b_ident], [b_bk], inc=(i == 3))
                            evac(xT[:, half * 4:(half + 1) * 4, j * 128:(j + 1) * 128],
                                 bk[:].rearrange("p (i t) -> p i t", i=4), [b_bk], [b_xT])
                    for qc in range(8):
                        bA, b_bA = nbank()
                        bB, b_bB = nbank()
                        for cc in range(8):
                            pe.op(lambda e: e.matmul(bA[:], wbf[:, cc, qc * 128:(qc + 1) * 128],
                                                     xT[:, cc, :], start=(cc == 0), stop=(cc == 7)),
                                  [b_wbf, b_xT], [b_bA], inc=(cc == 7))
                        for cc in range(8):
                            pe.op(lambda e: e.matmul(bB[:], wrot[:, cc, qc * 128:(qc + 1) * 128],
                                                     xT[:, cc, :], start=(cc == 0), stop=(cc == 7)),
                                  [b_wrot, b_xT], [b_bB], inc=(cc == 7))
                        t1, b_t1 = t1ring.next()
                        t2, b_t2 = t2ring.next()
                        qo, b_qo = qoring.next()
                        dve.op(lambda e: e.tensor_tensor(t1[:], bA[:], rc[:], ALU.mult),
                               [b_bA, b_rc], [b_t1])
                        dve.op(lambda e: e.tensor_tensor(t2[:], bB[:], rs[:], ALU.mult),
                               [b_bB, b_rs], [b_t2])
                        pool.op(lambda e: e.tensor_tensor(qo[:], t1[:], t2[:], ALU.add),
                                [b_t1, b_t2], [b_qo])
                        if qc < 4:
                            pool.dma(qT_s[qc, :, tsl], qo[:], [b_qo], [b_qT], sembuf=b_qo)
                        else:
                            pool.dma(kT_s[qc - 4, :, tsl], qo[:], [b_qo], [b_kT], sembuf=b_qo)
                    vo, b_vo = voring.next()
                    for j in range(4):
                        bk, b_bk = nbank()
                        for cc in range(8):
                            pe.op(lambda e: e.matmul(bk[:], xT[:, cc, j * 128:(j + 1) * 128],
                                                     wbf[:, cc, 1024:1536], start=(cc == 0), stop=(cc == 7)),
                                  [b_wbf, b_xT], [b_bk], inc=(cc == 7))
                        evac(vo[:, j, :], bk[:], [b_bk], [b_vo])
                    pool.dma(v_s[tsl, :].rearrange("(j p) e -> p j e", p=128), vo[:], [b_vo], [b_v],
                             sembuf=b_vo)
                    for rcn in range(8):
                        bk, b_bk = nbank()
                        for cc in range(8):
                            pe.op(lambda e: e.matmul(bk[:], wbf[:, cc, 1536 + rcn * 128:1536 + (rcn + 1) * 128],
                                                     xT[:, cc, :], start=(cc == 0), stop=(cc == 7)),
                                  [b_wbf, b_xT], [b_bk], inc=(cc == 7))
                        xo, b_xo = xgring.next()
                        evac(xo[:], bk[:], [b_bk], [b_xo])
                        pool.dma(xg_s[rcn, :, tsl], xo[:], [b_xo], [b_xg], sembuf=b_xo)
            c.barrier()

        def phase_lru(l):
            for cc in range(4):
                csl = slice(cc * 128, (cc + 1) * 128)
                with ExitStack() as es:
                    xr, b_xr = c.sb(es, "l_xr", [128, S], F32)
                    xc, b_xc = c.sb(es, "l_xc", [128, S], F32)
                    hh, b_hh = c.sb(es, "l_h", [128, S], F32)
                    rec, b_rec = c.sb(es, "l_rec", [128, S], BF16)
                    prm, b_prm = c.sb(es, "l_prm", [128, 16], F32)
                    ca, b_ca = c.sb(es, "l_ca", [128, 4], F32)
                    wbd = [[c.sb(es, "l_wbd", [128, 128], F32) for _ in range(2)] for _ in range(2)]
                    rr = c.ring(es, "l_r", [128, SBW], F32, 1)
                    ir = c.ring(es, "l_i", [128, SBW], F32, 1)
                    ar = c.ring(es, "l_a", [128, SBW], F32, 1)
                    a2r = c.ring(es, "l_a2", [128, SBW], F32, 1)
                    btr = c.ring(es, "l_bt", [128, SBW], F32, 1)
                    tmpr = c.ring(es, "l_tmp", [128, SBW], F32, 2)
                    sp.dma(xr[:], xg_s[cc, :, :], [b_xg], [b_xr])

                    def col(v):
                        return v.rearrange("(p o) -> p o", o=1)

                    cols = [conv_w[l, 0, csl], conv_w[l, 1, csl], conv_w[l, 2, csl], conv_w[l, 3, csl],
                            conv_b[l, csl], gate_a_b[l, 0, csl], gate_a_b[l, 1, csl],
                            gate_i_b[l, 0, csl], gate_i_b[l, 1, csl],
                            lru_lambda[l, 0, csl], lru_lambda[l, 1, csl]]
                    for i, v in enumerate(cols):
                        sp.dma(prm[:, i:i + 1], col(v), [], [b_prm])
                    act.op(lambda e: e.activation(ca[:, 0:2], prm[:, 9:11], AF.Exp, scale=-1.0),
                           [b_prm], [b_ca])
                    act.op(lambda e: e.activation(ca[:, 0:2], ca[:, 0:2], AF.Ln, bias=1.0),
                           [b_ca], [b_ca])
                    dve.op(lambda e: e.tensor_scalar(ca[:, 2:4], ca[:, 0:2], -16.0, None, ALU.mult),
                           [b_ca], [b_ca])
                    dve.op(lambda e: e.tensor_scalar(ca[:, 0:2], ca[:, 0:2], -8.0, None, ALU.mult),
                           [b_ca], [b_ca])
                    for d in range(2):
                        for gi, gw in enumerate((gate_a_w, gate_i_w)):
                            wt, b_wt = wbd[d][gi]
                            dve.op(lambda e: e.memset(wt[:], 0.0), [], [b_wt])
                            sp.dma(wt[0:64, 0:64], gw[l, d, 2 * cc], [], [b_wt])
                            sp.dma(wt[64:128, 64:128], gw[l, d, 2 * cc + 1], [], [b_wt])
                    dve.op(lambda e: e.tensor_scalar(xc[:], xr[:], prm[:, 2:3], prm[:, 4:5], ALU.mult, ALU.add),
                           [b_xr, b_prm], [b_xc])
                    dve.op(lambda e: e.scalar_tensor_tensor(xc[:, 2:S], xr[:, 0:S - 2], prm[:, 0:1], xc[:, 2:S],
                                                            ALU.mult, ALU.add), [b_xr, b_prm, b_xc], [b_xc])
                    dve.op(lambda e: e.scalar_tensor_tensor(xc[:, 1:S], xr[:, 0:S - 1], prm[:, 1:2], xc[:, 1:S],
                                                            ALU.mult, ALU.add), [b_xr, b_prm, b_xc], [b_xc])
                    dve.op(lambda e: e.scalar_tensor_tensor(xc[:, 0:S - 1], xr[:, 1:S], prm[:, 3:4], xc[:, 0:S - 1],
                                                            ALU.mult, ALU.add), [b_xr, b_prm, b_xc], [b_xc])
                    bi = [0]
                    for d in range(2):
                        carry = None
                        order = range(NSB) if d == 0 else range(NSB - 1, -1, -1)
                        for sbi in order:
                            ssl = slice(sbi * SBW, (sbi + 1) * SBW)
                            rt, b_rt = rr.next()
                            it, b_it = ir.next()
                            for gi, (gt, b_gt, bcol) in enumerate(((rt, b_rt, 5 + d), (it, b_it, 7 + d))):
                                wt, b_wt = wbd[d][gi]
                                for blk in range(SBW // 512):
                                    bi[0] += 1
                                    bk, b_bk = banks[bi[0] % 8]
                                    pe.op(lambda e: e.matmul(bk[:], wt[:], xc[:, sbi * SBW + blk * 512:
                                                                                sbi * SBW + (blk + 1) * 512],
                                                             start=True, stop=True), [b_wt, b_xc], [b_bk])
                                    act.op(lambda e: e.activation(gt[:, blk * 512:(blk + 1) * 512], bk[:],
                                                                  AF.Sigmoid, bias=prm[:, bcol:bcol + 1]),
                                           [b_bk, b_prm], [b_gt])
                            at, b_at = ar.next()
                            a2t, b_a2t = a2r.next()
                            bt, b_bt = btr.next()
                            act.op(lambda e: e.activation(at[:], rt[:], AF.Exp, scale=ca[:, d:d + 1]),
                                   [b_rt, b_ca], [b_at])
                            act.op(lambda e: e.activation(a2t[:], rt[:], AF.Exp, scale=ca[:, 2 + d:3 + d]),
                                   [b_rt, b_ca], [b_a2t])
                            act.op(lambda e: e.activation(a2t[:], a2t[:], AF.Sqrt, bias=1.0, scale=-1.0),
                                   [b_a2t], [b_a2t])
                            dve.op(lambda e: e.tensor_tensor(bt[:], a2t[:], it[:], ALU.mult),
                                   [b_a2t, b_it], [b_bt])
                            dve.op(lambda e: e.tensor_tensor(bt[:], bt[:], xc[:, ssl], ALU.mult),
                                   [b_bt, b_xc], [b_bt])
                            init = 0.0 if carry is None else carry[0]
                            crd = [] if carry is None else [carry[1]]
                            if d == 0:
                                dve.op(lambda e: e.tensor_tensor_scan(hh[:, ssl], at[:], bt[:], init,
                                                                      ALU.mult, ALU.add),
                                       [b_at, b_bt] + crd, [b_hh])
                                carry = (hh[:, (sbi + 1) * SBW - 1:(sbi + 1) * SBW], b_hh)
                            else:
                                tm, b_tm = tmpr.next()
                                dve.op(lambda e: e.tensor_tensor_scan(tm[:, ::-1], at[:, ::-1], bt[:, ::-1], init,
                                                                      ALU.mult, ALU.add),
                                       [b_at, b_bt] + crd, [b_tm])
                                carry = (tm[:, 0:1], b_tm)
                                pool.op(lambda e: e.tensor_tensor(hh[:, ssl], hh[:, ssl], tm[:], ALU.add),
                                        [b_hh, b_tm], [b_hh])
                    sp.dma(xr[:], xg_s[4 + cc, :, :], [b_xg], [b_xr])
                    act.op(lambda e: e.activation(xr[:], xr[:], AF.Gelu_apprx_tanh), [b_xr], [b_xr])
                    dve.op(lambda e: e.tensor_tensor(rec[:], hh[:], xr[:], ALU.mult), [b_hh, b_xr], [b_rec])
                    pool.dma(cat_s[4 + cc, :, :], rec[:], [b_rec], [b_cat], sembuf=b_rec)
                c.barrier()

        def phase_attn(l):
            lambda_init = 0.8 - 0.6 * math.exp(-0.3 * l)
            with ExitStack() as es:
                lq, b_lq = c.sb(es, "a_lq", [128, 256], F32)
                lt, b_lt = c.sb(es, "a_lt", [128, 128], F32)
                ls, b_ls = c.sb(es, "a_ls", [128, 4], F32)
                gsc, b_gsc = c.sb(es, "a_gsc", [128, 1], F32)
                sp.dma(lq[:], lambda_qk[l].partition_broadcast(128), [], [b_lq])
                dve.op(lambda e: e.tensor_tensor(lt[:, 0:64], lq[:, 0:64], lq[:, 64:128], ALU.mult),
                       [b_lq], [b_lt])
                dve.op(lambda e: e.tensor_tensor(lt[:, 64:128], lq[:, 128:192], lq[:, 192:256], ALU.mult),
                       [b_lq], [b_lt])
                dve.op(lambda e: e.tensor_reduce(ls[:, 0:2], lt[:].rearrange("p (a b) -> p a b", a=2),
                                                 AX.X, ALU.add), [b_lt], [b_ls])
                act.op(lambda e: e.activation(ls[:, 0:2], ls[:, 0:2], AF.Exp), [b_ls], [b_ls])
                dve.op(lambda e: e.tensor_scalar(ls[:, 2:3], ls[:, 1:2], -lambda_init, None, ALU.add),
                       [b_ls], [b_ls])
                dve.op(lambda e: e.tensor_tensor(ls[:, 3:4], ls[:, 2:3], ls[:, 0:1], ALU.subtract),
                       [b_ls], [b_ls])
                sp.dma(gsc[:], subln_g[l].rearrange("(p o) -> p o", o=1), [], [b_gsc])
                dve.op(lambda e: e.tensor_scalar(gsc[:], gsc[:], 1.0 - lambda_init, None, ALU.mult),
                       [b_gsc], [b_gsc])
                neglam = ls[:, 3:4]

                qT, b_qTt = c.sb(es, "a_qT", [128, S], BF16)
                kT, b_kTt = c.sb(es, "a_kT", [128, S], BF16)
                vv, b_vv = c.sb(es, "a_v", [128, NKT, 128], BF16)
                pring = c.ring(es, "a_p", [128, 512], BF16, 3)
                f1 = c.ring(es, "a_f1", [128, 512], F32, 2)
                f2 = c.ring(es, "a_f2", [128, 512], F32, 2)
                f3 = c.ring(es, "a_f3", [128, 512], F32, 2)
                f4 = c.ring(es, "a_f4", [128, 512], F32, 2)
                ores = c.ring(es, "a_o", [128, 512], BF16, 2)
                psS = [banks[0], banks[1]]
                psO = [banks[2], banks[3]]
                psZ = [banks[4], banks[5]]
                psM = banks[6]
                for h in range(4):
                    sp.dma(qT[:], qT_s[h], [b_qT], [b_qTt])
                    sp.dma(kT[:], kT_s[h], [b_kT], [b_kTt])
                    sp.dma(vv[:], v_s[:, h * 128:(h + 1) * 128].rearrange("(t p) e -> p t e", p=128),
                           [b_v], [b_vv])
                    for qb in range(NTB):
                        qsl = slice(qb * 512, (qb + 1) * 512)
                        units = [(m, kt) for m in range(2) for kt in range(NKT)]

                        def qk(i):
                            m, kt = units[i]
                            bk, b_bk = psS[i % 2]
                            pe.op(lambda e: e.matmul(bk[:], kT[64 * m:64 * m + 64, kt * 128:(kt + 1) * 128],
                                                     qT[64 * m:64 * m + 64, qsl], start=True, stop=True),
                                  [b_kTt, b_qTt], [b_bk])

                        qk(0)
                        for i, (m, kt) in enumerate(units):
                            if i + 1 < len(units):
                                qk(i + 1)
                            bk, b_bk = psS[i % 2]
                            pt, b_pt = pring.next()
                            act.op(lambda e: e.activation(pt[:], bk[:], AF.Exp, scale=0.125), [b_bk], [b_pt])
                            last = (kt == NKT - 1)
                            bo, b_bo = psO[m]
                            bz, b_bz = psZ[m]
                            pe.op(lambda e: e.matmul(bo[:], vv[:, kt, :], pt[:], start=(kt == 0), stop=last),
                                  [b_vv, b_pt], [b_bo], inc=False)
                            pe.op(lambda e: e.matmul(bz[:], ones_bf[:], pt[:], start=(kt == 0), stop=last),
                                  [b_ones_bf, b_pt], [b_bz], inc=True)
                        r1, b_r1 = f1.next()
                        o1, b_o1 = f2.next()
                        r2, b_r2 = f3.next()
                        o2, b_o2 = f4.next()
                        dve.op(lambda e: e.reciprocal(r1[:], psZ[0][0][:]), [psZ[0][1]], [b_r1])
                        dve.op(lambda e: e.tensor_tensor(o1[:], psO[0][0][:], r1[:], ALU.mult),
                               [psO[0][1], b_r1], [b_o1])
                        dve.op(lambda e: e.reciprocal(r2[:], psZ[1][0][:]), [psZ[1][1]], [b_r2])
                        dve.op(lambda e: e.tensor_tensor(o2[:], psO[1][0][:], r2[:], ALU.mult),
                               [psO[1][1], b_r2], [b_o2])
                        dve.op(lambda e: e.scalar_tensor_tensor(o1[:], o2[:], neglam, o1[:], ALU.mult, ALU.add),
                               [b_o2, b_o1, b_ls], [b_o1])
                        pool.op(lambda e: e.tensor_tensor(r1[:], o1[:], o1[:], ALU.mult), [b_o1], [b_r1])
                        bm, b_bm = psM
                        pe.op(lambda e: e.matmul(bm[:], ones_f[:], r1[:], start=True, stop=True),
                              [b_ones_f, b_r1], [b_bm])
                        act.op(lambda e: e.activation(r2[:], bm[:], AF.Ln, bias=1e-5, scale=1.0 / 128.0),
                               [b_bm], [b_r2])
                        act.op(lambda e: e.activation(r2[:], r2[:], AF.Exp, scale=-0.5), [b_r2], [b_r2])
                        ob, b_ob = ores.next()
                        dve.op(lambda e: e.scalar_tensor_tensor(ob[:], o1[:], gsc[:, 0:1], r2[:], ALU.mult, ALU.mult),
                               [b_o1, b_gsc, b_r2], [b_ob])
                        pool.dma(cat_s[h, :, qsl], ob[:], [b_ob], [b_cat], sembuf=b_ob)
            c.barrier()

        def layer_norm(es_tiles, r, b_r, grep, b_grep, brep, b_brep):
            st, b_st, mv, b_mv = es_tiles
            for k in range(2):
                dve.op(lambda e: e.bn_stats(st[:, k, :], r[:, k * 512:(k + 1) * 512]), [b_r], [b_st])
            dve.op(lambda e: e.bn_aggr(mv[:, 0:2], st[:].rearrange("p a b -> p (a b)")), [b_st], [b_mv])
            act.op(lambda e: e.activation(mv[:, 2:3], mv[:, 1:2], AF.Ln, bias=LN_EPS), [b_mv], [b_mv])
            act.op(lambda e: e.activation(mv[:, 2:3], mv[:, 2:3], AF.Exp, scale=-0.5), [b_mv], [b_mv])
            dve.op(lambda e: e.tensor_scalar(r, r, mv[:, 0:1], mv[:, 2:3], ALU.subtract, ALU.mult),
                   [b_r, b_mv], [b_r])
            pool.op(lambda e: e.tensor_tensor(r, r, grep[:], ALU.mult), [b_r, b_grep], [b_r])
            pool.op(lambda e: e.tensor_tensor(r, r, brep[:], ALU.add), [b_r, b_brep], [b_r])

        def phase_outproj(l, xsrc, b_xsrc):
            with ExitStack() as es:
                wo, b_wo = c.sb(es, "o_w", [128, 8, D], BF16)
                g1, b_g1 = c.sb(es, "o_g", [128, D], F32)
                b1, b_b1 = c.sb(es, "o_b", [128, D], F32)
                sp.dma(g1[:], ln1_g[l].partition_broadcast(128), [], [b_g1])
                sp.dma(b1[:], ln1_b[l].partition_broadcast(128), [], [b_b1])
                wst = c.ring(es, "o_wst", [128, D], F32, 2)
                for cc in range(8):
                    st, b_st = wst.next()
                    sp.dma(st[:], w_out[l, cc * 128:(cc + 1) * 128, :], [], [b_st])
                    evac(wo[:, cc, :], st[:], [b_st], [b_wo])
                catr = c.ring(es, "o_cat", [128, 8, 512], BF16, 2)
                xr_ = c.ring(es, "o_x", [128, 4, D], F32, 2)
                rr_ = c.ring(es, "o_r", [128, D], F32, 3)
                stt = c.ring(es, "o_st", [128, 2, 6], F32, 2)
                mvt = c.ring(es, "o_mv", [128, 4], F32, 2)
                bi = [0]
                for tb in range(NTB):
                    tsl = slice(tb * 512, (tb + 1) * 512)
                    ct, b_ct = catr.next()
                    sp.dma(ct[:], cat_s[:, :, tsl].rearrange("c p t -> p c t"), [b_cat], [b_ct])
                    xt, b_xt = xr_.next()
                    sp.dma(xt[:], xsrc[tsl, :].rearrange("(j p) d -> p j d", p=128), [b_xsrc], [b_xt])
                    for j in range(4):
                        r, b_r = rr_.next()
                        for hf in range(2):
                            bi[0] += 1
                            bk, b_bk = banks[bi[0] % 8]
                            for cc in range(8):
                                pe.op(lambda e: e.matmul(bk[:], ct[:, cc, j * 128:(j + 1) * 128],
                                                         wo[:, cc, hf * 512:(hf + 1) * 512],
                                                         start=(cc == 0), stop=(cc == 7)),
                                      [b_ct, b_wo], [b_bk], inc=(cc == 7))
                            dve.op(lambda e: e.scalar_tensor_tensor(r[:, hf * 512:(hf + 1) * 512],
                                                                    xt[:, j, hf * 512:(hf + 1) * 512], ALPHA,
                                                                    bk[:], ALU.mult, ALU.add),
                                   [b_xt, b_bk], [b_r])
                        st, b_st = stt.next()
                        mv, b_mv = mvt.next()
                        layer_norm((st, b_st, mv, b_mv), r[:], b_r, g1, b_g1, b1, b_b1)
                        pool.dma(x1_s[tb * 512 + j * 128:tb * 512 + (j + 1) * 128, :], r[:], [b_r], [b_x1s],
                                 sembuf=b_r)
            c.barrier()

        def phase_peer(l, ydst, b_ydst):
            with ExitStack() as es:
                wq, b_wq = c.sb(es, "e_wq", [128, 8, 2048], BF16)
                kTt, b_kTt = c.sb(es, "e_keysT", [128, 2, 128], BF16)
                g2, b_g2 = c.sb(es, "e_g", [128, D], F32)
                b2, b_b2 = c.sb(es, "e_b", [128, D], F32)
                sp.dma(g2[:], ln2_g[l].partition_broadcast(128), [], [b_g2])
                sp.dma(b2[:], ln2_b[l].partition_broadcast(128), [], [b_b2])
                with ExitStack() as es2:
                    wst = c.ring(es2, "e_wst", [128, 2048], F32, 2)
                    for cc in range(8):
                        st, b_st = wst.next()
                        sp.dma(st[:], peer_wq[l, cc * 128:(cc + 1) * 128, :], [], [b_st])
                        evac(wq[:, cc, :], st[:], [b_st], [b_wq])
                    for p in range(2):
                        st, b_st = wst.next()
                        sp.dma(st[:, 0:128], peer_keys[l, p], [], [b_st])
                        bk, b_bk = banks[p]
                        pe.op(lambda e: e.transpose(bk[:, 0:128], st[:, 0:128], ident[:]), [b_st, b_ident], [b_bk])
                        evac(kTt[:, p, :], bk[:, 0:128], [b_bk], [b_kTt])
                    c.barrier()
                GT, b_GT = c.sb(es, "e_GT", [128, 128, TG], BF16)
                x1, b_x1 = c.sb(es, "e_x1", [128, 2, D], F32)
                x1T, b_x1T = c.sb(es, "e_x1T", [128, 8, TG], BF16)
                qpT, b_qpT = c.sb(es, "e_qpT", [128, 16, TG], BF16)
                bA, b_bA = c.sb(es, "e_bA", [128, 2048], F32)
                bB, b_bB = c.sb(es, "e_bB", [128, 2048], F32)
                bC, b_bC = c.sb(es, "e_bC", [128, 2048], F32)
                stop_, b_stop = c.sb(es, "e_stop", [128, 16, 16], F32)
                itop, b_itop = c.sb(es, "e_itop", [128, 16, 16], U32)
                itopf, b_itopf = c.sb(es, "e_itopf", [128, 16, 16], F32)
                fs, b_fs = c.sb(es, "e_fs", [128, 8, 16], F32)
                fi, b_fi = c.sb(es, "e_fi", [128, 8, 16], U32)
                fhi, b_fhi = c.sb(es, "e_fhi", [128, 8, 16], U32)
                flo, b_flo = c.sb(es, "e_flo", [128, 8, 16], U32)
                fhf, b_fhf = c.sb(es, "e_fhf", [128, 8, 16], F32)
                flf, b_flf = c.sb(es, "e_flf", [128, 8, 16], F32)
                e12g, b_e12g = c.sb(es, "e_e12g", [128, 3, 128], F32)
                sm, b_sm = c.sb(es, "e_sm", [128, 16], F32)
                eT, b_eT = c.sb(es, "e_eT", [128, 3, TG], F32)
                Lr = c.ring(es, "e_L", [128, 128], BF16, 4)
                Rr = c.ring(es, "e_R", [128, 128], BF16, 4)
                eur = c.ring(es, "e_eu", [128, 2, 1024], BF16, 3)
                evr = c.ring(es, "e_ev", [128, 2, 1024], BF16, 3)
                glr = c.ring(es, "e_gl", [128, TG], BF16, 3)
                Hr = c.ring(es, "e_H", [128, TG], BF16, 3)
                stt = c.ring(es, "e_st", [128, 2, 6], F32, 2)
                mvt = c.ring(es, "e_mv", [128, 4], F32, 2)

                for g in range(NG):
                    t0 = g * TG
                    sp.dma(x1[:], x1_s[t0:t0 + TG, :].rearrange("(j p) d -> p j d", p=128), [b_x1s], [b_x1])
                    for j in range(2):
                        for half in range(2):
                            bk, b_bk = banks[2 + (j * 2 + half) % 2]
                            for i in range(4):
                                cc = half * 4 + i
                                pe.op(lambda e: e.transpose(bk[:, i * 128:(i + 1) * 128],
                                                            x1[:, j, cc * 128:(cc + 1) * 128], ident[:]),
                                      [b_x1, b_ident], [b_bk], inc=(i == 3))
                            evac(x1T[:, half * 4:(half + 1) * 4, j * 128:(j + 1) * 128],
                                 bk[:].rearrange("p (i t) -> p i t", i=4), [b_bk], [b_x1T])
                    for hp in range(16):
                        bk, b_bk = banks[2 + hp % 2]
                        for cc in range(8):
                            pe.op(lambda e: e.matmul(bk[:, 0:TG], wq[:, cc, hp * 128:(hp + 1) * 128], x1T[:, cc, :],
                                                     start=(cc == 0), stop=(cc == 7)),
                                  [b_wq, b_x1T], [b_bk], inc=(cc == 7))
                        evac(qpT[:, hp, :], bk[:, 0:TG], [b_bk], [b_qpT])
                    for j in range(2):
                        jsl = slice(j * 128, (j + 1) * 128)
                        for q4 in range(4):
                            bk, b_bk = banks[4 + q4]
                            for i in range(4):
                                hp = q4 * 4 + i
                                pe.op(lambda e: e.matmul(bk[:, i * 128:(i + 1) * 128], qpT[:, hp, jsl],
                                                         kTt[:, hp % 2, :], start=True, stop=True),
                                      [b_qpT, b_kTt], [b_bk], inc=(i == 3))
                            evac(bA[:, q4 * 512:(q4 + 1) * 512], bk[:], [b_bk], [b_bA])
                        for hp in range(16):
                            ksl = slice(hp * 128, (hp + 1) * 128)
                            dve.op(lambda e: e.max(stop_[:, hp, 0:8], bA[:, ksl]), [b_bA], [b_stop])
                            dve.op(lambda e: e.max_index(itop[:, hp, 0:8], stop_[:, hp, 0:8], bA[:, ksl]),
                                   [b_bA, b_stop], [b_itop])
                            dve.op(lambda e: e.match_replace(bB[:, ksl], stop_[:, hp, 0:8], bA[:, ksl], NEG),
                                   [b_bA, b_stop], [b_bB])
                            dve.op(lambda e: e.max(stop_[:, hp, 8:16], bB[:, ksl]), [b_bB], [b_stop])
                            dve.op(lambda e: e.max_index(itop[:, hp, 8:16], stop_[:, hp, 8:16], bB[:, ksl]),
                                   [b_bB, b_stop], [b_itop])
                        dve.op(lambda e: e.tensor_copy(itopf[:], itop[:]), [b_itop], [b_itopf])
                        for h in range(8):
                            in0 = stop_[:, 2 * h, :].unsqueeze(2).to_broadcast([128, 16, 16])
                            in1 = stop_[:, 2 * h + 1, :].unsqueeze(1).to_broadcast([128, 16, 16])
                            dve.op(lambda e: e.tensor_tensor(
                                bC[:, h * 256:(h + 1) * 256].rearrange("p (a b) -> p a b", b=16),
                                in0, in1, ALU.add), [b_stop], [b_bC])
                        for h in range(8):
                            ksl = slice(h * 256, (h + 1) * 256)
                            dve.op(lambda e: e.max(fs[:, h, 0:8], bC[:, ksl]), [b_bC], [b_fs])
                            dve.op(lambda e: e.max_index(fi[:, h, 0:8], fs[:, h, 0:8], bC[:, ksl]),
                                   [b_bC, b_fs], [b_fi])
                            dve.op(lambda e: e.match_replace(bB[:, ksl], fs[:, h, 0:8], bC[:, ksl], NEG),
                                   [b_bC, b_fs], [b_bB])
                            dve.op(lambda e: e.max(fs[:, h, 8:16], bB[:, ksl]), [b_bB], [b_fs])
                            dve.op(lambda e: e.max_index(fi[:, h, 8:16], fs[:, h, 8:16], bB[:, ksl]),
                                   [b_bB, b_fs], [b_fi])
                        dve.op(lambda e: e.tensor_single_scalar(fhi[:], fi[:], 4, ALU.logical_shift_right),
                               [b_fi], [b_fhi])
                        dve.op(lambda e: e.tensor_single_scalar(flo[:], fi[:], 15, ALU.bitwise_and),
                               [b_fi], [b_flo])
                        dve.op(lambda e: e.tensor_copy(fhf[:], fhi[:]), [b_fhi], [b_fhf])
                        dve.op(lambda e: e.tensor_copy(flf[:], flo[:]), [b_flo], [b_flf])
                        for which, (src, b_src) in enumerate(((fhf, b_fhf), (flf, b_flf))):
                            for h in range(8):
                                oh = bA[:, h * 256:(h + 1) * 256].rearrange("p (n i) -> p n i", i=16)
                                dve.op(lambda e: e.tensor_tensor(
                                    oh, src[:, h, :].unsqueeze(2).to_broadcast([128, 16, 16]),
                                    iota[:, 0:16].unsqueeze(1).to_broadcast([128, 16, 16]), ALU.is_equal),
                                    [b_src, b_iota], [b_bA])
                                dve.op(lambda e: e.tensor_tensor(
                                    oh, oh, itopf[:, 2 * h + which, :].unsqueeze(1).to_broadcast([128, 16, 16]),
                                    ALU.mult), [b_bA, b_itopf], [b_bA])
                            dve.op(lambda e: e.tensor_reduce(
                                e12g[:, which, :], bA[:].rearrange("p (n i) -> p n i", i=16), AX.X, ALU.add),
                                [b_bA], [b_e12g])
                        dve.op(lambda e: e.tensor_tensor(
                            bB[:, 0:128].rearrange("p (h n) -> p h n", n=16), fs[:],
                            fs[:, :, 0:1].to_broadcast([128, 8, 16]), ALU.subtract), [b_fs], [b_bB])
                        act.op(lambda e: e.activation(bB[:, 0:128], bB[:, 0:128], AF.Exp), [b_bB], [b_bB])
                        dve.op(lambda e: e.tensor_reduce(sm[:, 0:8], bB[:, 0:128].rearrange("p (h n) -> p h n", n=16),
                                                         AX.X, ALU.add), [b_bB], [b_sm])
                        dve.op(lambda e: e.reciprocal(sm[:, 8:16], sm[:, 0:8]), [b_sm], [b_sm])
                        dve.op(lambda e: e.tensor_tensor(
                            e12g[:, 2, :].rearrange("p (h n) -> p h n", n=16),
                            bB[:, 0:128].rearrange("p (h n) -> p h n", n=16),
                            sm[:, 8:16].unsqueeze(2).to_broadcast([128, 8, 16]), ALU.mult),
                            [b_bB, b_sm], [b_e12g])
                        bk, b_bk = banks[2 + j % 2]
                        for w3 in range(3):
                            pe.op(lambda e: e.transpose(bk[:, w3 * 128:(w3 + 1) * 128], e12g[:, w3, :], ident[:]),
                                  [b_e12g, b_ident], [b_bk], inc=(w3 == 2))
                        evac(eT[:, :, jsl], bk[:, 0:384].rearrange("p (w t) -> p w t", w=3), [b_bk], [b_eT])
                    for t4 in range(TG // 4):
                        bk, b_bk = banks[2 + t4 % 2]
                        for i in range(4):
                            t = t4 * 4 + i
                            Lt, b_Lt = Lr.next()
                            Rt, b_Rt = Rr.next()
                            dve.op(lambda e: e.tensor_scalar(Lt[:], iota[:], eT[:, 0, t:t + 1], eT[:, 2, t:t + 1],
                                                             ALU.is_equal, ALU.mult), [b_iota, b_eT], [b_Lt])
                            pool.op(lambda e: e.tensor_scalar(Rt[:], iota[:], eT[:, 1, t:t + 1], None,
                                                              ALU.is_equal), [b_iota, b_eT], [b_Rt])
                            pe.op(lambda e: e.matmul(bk[:, i * 128:(i + 1) * 128], Rt[:], Lt[:],
                                                     start=True, stop=True), [b_Rt, b_Lt], [b_bk], inc=(i == 3))
                        act.op(lambda e: e.copy(GT[:, :, t4 * 4:(t4 + 1) * 4].rearrange("p k t -> p t k"),
                                                bk[:].rearrange("p (t k) -> p t k", t=4)), [b_bk], [b_GT])
                    psF = [[banks[4], banks[5]], [banks[6], banks[7]]]
                    for k1 in range(128):
                        if k1 % 2 == 0:
                            eu, b_eu_t = eur.next()
                            ev, b_ev_t = evr.next()
                            sp.dma(eu[:], euT_s[l, k1:k1 + 2].rearrange("k p f -> p k f"), [b_eu], [b_eu_t])
                            sp.dma(ev[:], ev_s[l, k1:k1 + 2].rearrange("k p f -> p k f"), [b_ev], [b_ev_t])
                        kk = k1 % 2
                        bu, b_bu = banks[2 + k1 % 2]
                        for cc in range(8):
                            pe.op(lambda e: e.matmul(bu[:, 0:TG], eu[:, kk, cc * 128:(cc + 1) * 128], x1T[:, cc, :],
                                                     start=(cc == 0), stop=(cc == 7)),
                                  [b_eu_t, b_x1T], [b_bu], inc=(cc == 7))
                        gl, b_gl = glr.next()
                        act.op(lambda e: e.activation(gl[:], bu[:, 0:TG], AF.Gelu_apprx_tanh), [b_bu], [b_gl])
                        Ht, b_Ht = Hr.next()
                        dve.op(lambda e: e.tensor_tensor(Ht[:], gl[:], GT[:, k1, :], ALU.mult),
                               [b_gl, b_GT], [b_Ht])
                        for j in range(2):
                            for hf in range(2):
                                bf, b_bf = psF[j][hf]
                                pe.op(lambda e: e.matmul(bf[:], Ht[:, j * 128:(j + 1) * 128],
                                                         ev[:, kk, hf * 512:(hf + 1) * 512],
                                                         start=(k1 == 0), stop=(k1 == 127)),
                                      [b_Ht, b_ev_t], [b_bf], inc=(j == 1 and hf == 1))
                    for j in range(2):
                        for hf in range(2):
                            bf, b_bf = psF[j][hf]
                            dve.op(lambda e: e.scalar_tensor_tensor(x1[:, j, hf * 512:(hf + 1) * 512],
                                                                    x1[:, j, hf * 512:(hf + 1) * 512], ALPHA,
                                                                    bf[:], ALU.mult, ALU.add),
                                   [b_x1, b_bf], [b_x1])
                        st, b_st = stt.next()
                        mv, b_mv = mvt.next()
                        layer_norm((st, b_st, mv, b_mv), x1[:, j, :], b_x1, g2, b_g2, b2, b_b2)
                    pool.dma(ydst[t0:t0 + TG, :].rearrange("(j p) d -> p j d", p=128), x1[:], [b_x1], [b_ydst],
                             sembuf=b_x1)
            c.barrier()

        phase_tables()
        for slot in range(NSLOT):
            for l in range(DEPTH):
                if l == 0:
                    xsrc, b_xsrc = x_in[slot], Buf("xin")
                else:
                    xsrc, b_xsrc = xmid, b_xmid
                if l == DEPTH - 1:
                    ydst, b_ydst = y_out[slot], b_y
                else:
                    ydst, b_ydst = xmid, b_xmid
                phase_proj(l, xsrc, b_xsrc)
                phase_lru(l)
                phase_attn(l)
                phase_outproj(l, xsrc, b_xsrc)
                phase_peer(l, ydst, b_ydst)
        c.barrier()
    return nc


def _consts(S):
    ident = np.eye(128, dtype=np.float32)
    iota = np.tile(np.arange(128, dtype=np.float32)[None, :], (128, 1))
    inv = (np.float32(500000.0) ** (-np.arange(0, 16, 2, dtype=np.float32) / np.float32(16))).astype(np.float32)
    ang = (np.arange(S, dtype=np.float32)[:, None] * inv[None, :]).astype(np.float32)
    cos = np.cos(ang).astype(np.float32).T
    sin = np.sin(ang).astype(np.float32).T
    rotc = np.ones((128, S), np.float32)
    rots = np.zeros((128, S), np.float32)
    for blk in range(2):
        o = blk * 64
        rotc[o:o + 8] = cos
        rotc[o + 8:o + 16] = cos
        rots[o:o + 8] = sin
        rots[o + 8:o + 16] = sin
    return {"c_ident": ident, "c_iota": iota, "c_rotc": rotc, "c_rots": rots}


_W_KEYS = ["w_in", "lambda_qk", "subln_g", "conv_w", "conv_b", "gate_a_w", "gate_a_b", "gate_i_w", "gate_i_b",
           "lru_lambda", "w_out", "ln1_g", "ln1_b", "peer_wq", "peer_keys", "expert_u", "expert_v", "ln2_g", "ln2_b"]


def _weights_map(inputs, depth):
    m = {}
    for k in _W_KEYS:
        a = np.ascontiguousarray(np.asarray(inputs[k], dtype=np.float32))
        if k == "lambda_qk":
            a = a.reshape(depth, 256)
        m[k] = a
    return m


def kernel(**inputs):
    xp = np.asarray(inputs["x_prompt"], dtype=np.float32)
    xs = np.asarray(inputs["x_sample"], dtype=np.float32)
    S = xp.shape[1]
    depth = inputs["w_in"].shape[0]
    seqs = [xp[i] for i in range(xp.shape[0])] + [xs[i] for i in range(xs.shape[0])]
    n_cores = 8
    nslot = (len(seqs) + n_cores - 1) // n_cores
    nc = build_nc(S=S, NSLOT=nslot, DEPTH=depth)
    wm = _weights_map(inputs, depth)
    cm = _consts(S)
    in_maps = []
    assign = []
    for core in range(n_cores):
        ids = []
        for s in range(nslot):
            i = core + s * n_cores
            ids.append(i if i < len(seqs) else core)
        assign.append(ids)
        xcore = np.ascontiguousarray(np.stack([seqs[i] for i in ids], axis=0))
        mp = {"x": xcore}
        mp.update(wm)
        mp.update(cm)
        in_maps.append(mp)
    res = run_bass_kernel_spmd(nc, in_maps, core_ids=list(range(n_cores)))
    outs = [None] * len(seqs)
    for core in range(n_cores):
        y = res.results[core]["y"]
        for s in range(nslot):
            i = core + s * n_cores
            if i < len(seqs):
                outs[i] = np.asarray(y[s], dtype=np.float32)
    y_prompt = np.stack(outs[:xp.shape[0]], axis=0)
    y_sample = np.stack(outs[xp.shape[0]:], axis=0)
    return (y_prompt, y_sample)
```

```python
import math
from contextlib import ExitStack

import numpy as np
import concourse.bass as bass
import concourse.mybir as mybir
from concourse.bass_utils import run_bass_kernel_spmd

F32 = mybir.dt.float32
BF16 = mybir.dt.bfloat16
U32 = mybir.dt.uint32
AF = mybir.ActivationFunctionType
ALU = mybir.AluOpType
AX = mybir.AxisListType

D = 1024
INW = 2560
NE = 16384
ALPHA = 4.0 ** 0.25
LN_EPS = 1e-5
NEG = -1.0e30


class SemRef:
    __slots__ = ("h", "count", "name")

    def __init__(self, h, name):
        self.h = h
        self.count = 0
        self.name = name


class Buf:
    __slots__ = ("name", "w", "rs", "dsem")

    def __init__(self, name=""):
        self.name = name
        self.w = None
        self.rs = {}
        self.dsem = None


class Eng:
    def __init__(self, ctx, name, eng, self_sync=True):
        self.ctx = ctx
        self.name = name
        self.eng = eng
        self.sem = ctx.new_sem("e_" + name)
        self.seen = {}
        self.self_sync = self_sync

    def _waits(self, reads, writes, extra=()):
        need = {}
        for b in reads:
            if b.w is not None:
                k, v = b.w
                if need.get(k, 0) < v:
                    need[k] = v
        for b in writes:
            if b.w is not None:
                k, v = b.w
                if need.get(k, 0) < v:
                    need[k] = v
            for k, v in b.rs.items():
                if need.get(k, 0) < v:
                    need[k] = v
        for k, v in extra:
            if need.get(k, 0) < v:
                need[k] = v
        for k, v in need.items():
            if k is self.sem and not self.self_sync:
                continue
            if self.seen.get(k, 0) >= v:
                continue
            self.eng.wait_ge(k.h, v)
            self.seen[k] = v

    def op(self, fn, reads=(), writes=(), inc=True):
        self._waits(reads, writes)
        ins = fn(self.eng)
        tok = (self.sem, self.sem.count + 1)
        if inc:
            ins.then_inc(self.sem.h, 1)
            self.sem.count += 1
        for b in reads:
            if b.rs.get(tok[0], 0) < tok[1]:
                b.rs[tok[0]] = tok[1]
        for b in writes:
            b.w = tok
            b.rs = {}
        return ins

    def dma(self, out, in_, reads, writes, sembuf=None, **kw):
        sb = sembuf if sembuf is not None else (writes[0] if writes else reads[0])
        if sb.dsem is None:
            sb.dsem = self.ctx.get_dsem()
        ds = sb.dsem
        extra = [(ds, ds.count)] if ds.count > 0 else []
        self._waits(reads, writes, extra)
        ins = self.eng.dma_start(out=out, in_=in_, **kw)
        ins.then_inc(ds.h, 16)
        ds.count += 16
        tok = (ds, ds.count)
        for b in reads:
            if b.rs.get(tok[0], 0) < tok[1]:
                b.rs[tok[0]] = tok[1]
        for b in writes:
            b.w = tok
            b.rs = {}
        return ins


class Ctx:
    def __init__(self, nc):
        self.nc = nc
        self.es = ExitStack()
        self.nsem = 0
        self.allsems = []
        self.free_dsems = []
        self.phase_dsems = []
        self.uid = 0
        self.pe = Eng(self, "pe", nc.tensor, self_sync=False)
        self.act = Eng(self, "act", nc.scalar)
        self.dve = Eng(self, "dve", nc.vector)
        self.pool = Eng(self, "pool", nc.gpsimd)
        self.sp = Eng(self, "sp", nc.sync)
        self.engs = [self.pe, self.act, self.dve, self.pool, self.sp]

    def new_sem(self, name):
        self.nsem += 1
        h = self.es.enter_context(self.nc.semaphore("%s_%d" % (name, self.nsem)))
        s = SemRef(h, name)
        self.allsems.append(s)
        return s

    def get_dsem(self):
        if self.free_dsems:
            s = self.free_dsems.pop()
        else:
            s = self.new_sem("d")
        self.phase_dsems.append(s)
        return s

    def barrier(self):
        for e in self.engs:
            for s in self.allsems:
                if s.count > 0 and e.seen.get(s, 0) < s.count and not (s is e.sem):
                    e.eng.wait_ge(s.h, s.count)
                    e.seen[s] = s.count
        self.free_dsems.extend(self.phase_dsems)
        self.phase_dsems = []

    def sb(self, es, name, shape, dtype):
        self.uid += 1
        t = es.enter_context(self.nc.sbuf_tensor("%s_%d" % (name, self.uid), list(shape), dtype))
        return t, Buf(name)

    def ring(self, es, name, shape, dtype, n):
        return Ring([self.sb(es, name, shape, dtype) for _ in range(n)])


class Ring:
    def __init__(self, items):
        self.items = items
        self.i = 0

    def next(self):
        it = self.items[self.i % len(self.items)]
        self.i += 1
        return it


def build_nc(S=8192, NSLOT=2, DEPTH=2, dbg=False, phases=None):
    nc = bass.Bass("TRN2", target_bir_lowering=False)
    NTB = S // 512
    NKT = S // 128
    SBW = min(S, 2048)
    NSB = S // SBW
    TG = 256
    NG = S // TG

    def din(name, shape, dt=F32):
        return nc.dram_tensor(name, list(shape), dt, kind="ExternalInput").ap()

    def dscr(name, shape, dt):
        kind = "ExternalOutput" if dbg else "Internal"
        return nc.dram_tensor(name, list(shape), dt, kind=kind).ap()

    x_in = din("x", [NSLOT, S, D])
    w_in = din("w_in", [DEPTH, D, INW])
    lambda_qk = din("lambda_qk", [DEPTH, 256])
    subln_g = din("subln_g", [DEPTH, 128])
    conv_w = din("conv_w", [DEPTH, 4, 512])
    conv_b = din("conv_b", [DEPTH, 512])
    gate_a_w = din("gate_a_w", [DEPTH, 2, 8, 64, 64])
    gate_a_b = din("gate_a_b", [DEPTH, 2, 512])
    gate_i_w = din("gate_i_w", [DEPTH, 2, 8, 64, 64])
    gate_i_b = din("gate_i_b", [DEPTH, 2, 512])
    lru_lambda = din("lru_lambda", [DEPTH, 2, 512])
    w_out = din("w_out", [DEPTH, D, D])
    ln1_g = din("ln1_g", [DEPTH, D])
    ln1_b = din("ln1_b", [DEPTH, D])
    peer_wq = din("peer_wq", [DEPTH, D, 2048])
    peer_keys = din("peer_keys", [DEPTH, 2, 128, 128])
    expert_u = din("expert_u", [DEPTH, NE, D])
    expert_v = din("expert_v", [DEPTH, NE, D])
    ln2_g = din("ln2_g", [DEPTH, D])
    ln2_b = din("ln2_b", [DEPTH, D])
    ident_d = din("c_ident", [128, 128])
    rotc_d = din("c_rotc", [128, S])
    rots_d = din("c_rots", [128, S])
    iota_d = din("c_iota", [128, 128])

    y_out = nc.dram_tensor("y", [NSLOT, S, D], F32, kind="ExternalOutput").ap()

    xmid = dscr("s_xmid", [S, D], F32)
    x1_s = dscr("s_x1", [S, D], F32)
    qT_s = dscr("s_qT", [4, 128, S], BF16)
    kT_s = dscr("s_kT", [4, 128, S], BF16)
    v_s = dscr("s_v", [S, 512], BF16)
    xg_s = dscr("s_xg", [8, 128, S], F32)
    cat_s = dscr("s_cat", [8, 128, S], BF16)
    euT_s = dscr("s_euT", [DEPTH, 128, 128, 1024], BF16)
    ev_s = dscr("s_ev", [DEPTH, 128, 128, 1024], BF16)
    b_xmid, b_x1s, b_qT, b_kT, b_v, b_xg, b_cat = (Buf(n) for n in
                                                    ("xmid", "x1s", "qT", "kT", "v", "xg", "cat"))
    b_eu, b_ev, b_y = Buf("euT"), Buf("evs"), Buf("y")

    c = Ctx(nc)
    pe, act, dve, pool, sp = c.pe, c.act, c.dve, c.pool, c.sp

    with c.es, ExitStack() as ges:
        psum_all = ges.enter_context(nc.psum_tensor("psum_all", [128, 8, 512], F32))
        banks = [(psum_all[:, i, :], Buf("bank%d" % i)) for i in range(8)]
        ident, b_ident = c.sb(ges, "ident", [128, 128], F32)
        iota, b_iota = c.sb(ges, "iota", [128, 128], F32)
        ones_bf, b_ones_bf = c.sb(ges, "ones_bf", [128, 128], BF16)
        ones_f, b_ones_f = c.sb(ges, "ones_f", [128, 128], F32)
        sp.dma(ident[:], ident_d[:, :], [], [b_ident])
        sp.dma(iota[:], iota_d[:, :], [], [b_iota])
        dve.op(lambda e: e.memset(ones_bf[:], 1.0), [], [b_ones_bf])
        dve.op(lambda e: e.memset(ones_f[:], 1.0), [], [b_ones_f])
        iota_b, b_iota_b = c.sb(ges, "iota_b", [128, 128], BF16)
        dve.op(lambda e: e.tensor_copy(iota_b[:], iota[:]), [b_iota], [b_iota_b])

        alt = [0]

        def evac(out_ap, in_ap, reads, writes):
            alt[0] += 1
            if alt[0] % 2:
                dve.op(lambda e: e.tensor_copy(out_ap, in_ap), reads, writes)
            else:
                act.op(lambda e: e.copy(out_ap, in_ap), reads, writes)

        def phase_tables():
            with ExitStack() as es:
                ld = c.ring(es, "t_ld", [128, 1024], F32, 4)
                euo = c.ring(es, "t_euo", [128, 8, 128], BF16, 2)
                evo = c.ring(es, "t_evo", [128, 1024], BF16, 2)
                for l in range(DEPTH):
                    for k1 in range(128):
                        ut, b_ut = ld.next()
                        sp.dma(ut[:], expert_u[l, k1 * 128:(k1 + 1) * 128, :], [], [b_ut])
                        vt, b_vt = ld.next()
                        sp.dma(vt[:], expert_v[l, k1 * 128:(k1 + 1) * 128, :], [], [b_vt])
                        eo, b_eo = euo.next()
                        for half in range(2):
                            bk, b_bk = banks[(k1 * 2 + half) % 4]
                            for i in range(4):
                                cc = half * 4 + i
                                pe.op(lambda e: e.transpose(bk[:, i * 128:(i + 1) * 128],
                                                            ut[:, cc * 128:(cc + 1) * 128], ident[:]),
                                      [b_ut, b_ident], [b_bk], inc=(i == 3))
                            evac(eo[:, half * 4:(half + 1) * 4, :],
                                 bk[:].rearrange("p (i k) -> p i k", i=4), [b_bk], [b_eo])
                        pool.dma(euT_s[l, k1].rearrange("p (c k) -> p c k", c=8), eo[:],
                                 [b_eo], [b_eu], sembuf=b_eo)
                        vo, b_vo = evo.next()
                        if k1 % 2:
                            act.op(lambda e: e.copy(vo[:], vt[:]), [b_vt], [b_vo])
                        else:
                            pool.op(lambda e: e.tensor_copy(vo[:], vt[:]), [b_vt], [b_vo])
                        pool.dma(ev_s[l, k1], vo[:], [b_vo], [b_ev], sembuf=b_vo)
            c.barrier()

        def phase_proj(l, xsrc, b_xsrc):
            with ExitStack() as es:
                wbf, b_wbf = c.sb(es, "wbf", [128, 8, INW], BF16)
                wrot, b_wrot = c.sb(es, "wrot", [128, 8, 1024], BF16)
                with ExitStack() as es2:
                    wst = c.ring(es2, "wst", [128, INW], F32, 2)
                    for cc in range(8):
                        st, b_st = wst.next()
                        sp.dma(st[:], w_in[l, cc * 128:(cc + 1) * 128, :], [], [b_st])
                        if cc % 2:
                            act.op(lambda e: e.copy(wbf[:, cc, :], st[:]), [b_st], [b_wbf])
                        else:
                            dve.op(lambda e: e.tensor_copy(wbf[:, cc, :], st[:]), [b_st], [b_wbf])
                    pool.op(lambda e: e.memset(wrot[:], 0.0), [], [b_wrot])
                    for cc in range(8):
                        src = wbf[:, cc, 0:1024].rearrange("p (b j) -> p b j", j=64)
                        dst = wrot[:, cc, :].rearrange("p (b j) -> p b j", j=64)
                        dve.op(lambda e: e.tensor_scalar(dst[:, :, 0:8], src[:, :, 8:16], -1.0, None,
                                                         ALU.mult), [b_wbf], [b_wrot])
                        dve.op(lambda e: e.tensor_copy(dst[:, :, 8:16], src[:, :, 0:8]),
                               [b_wbf], [b_wrot])
                    c.barrier()
                xring = c.ring(es, "p_x", [128, 4, D], F32, 2)
                xTring = c.ring(es, "p_xT", [128, 8, 512], BF16, 2)
                rcring = c.ring(es, "p_rc", [128, 512], F32, 2)
                rsring = c.ring(es, "p_rs", [128, 512], F32, 2)
                t1ring = c.ring(es, "p_t1", [128, 512], F32, 2)
                t2ring = c.ring(es, "p_t2", [128, 512], F32, 2)
                qoring = c.ring(es, "p_qo", [128, 512], BF16, 3)
                voring = c.ring(es, "p_vo", [128, 4, 512], BF16, 2)
                xgring = c.ring(es, "p_xg", [128, 512], F32, 3)
                bi = [0]

                def nbank():
                    bi[0] += 1
                    return banks[bi[0] % 8]

                for tb in range(NTB):
                    tsl = slice(tb * 512, (tb + 1) * 512)
                    xt, b_xt = xring.next()
                    sp.dma(xt[:], xsrc[tsl, :].rearrange("(j p) d -> p j d", p=128), [b_xsrc], [b_xt])
                    rc, b_rc = rcring.next()
                    sp.dma(rc[:], rotc_d[:, tsl], [], [b_rc])
                    rs, b_rs = rsring.next()
                    sp.dma(rs[:], rots_d[:, tsl], [], [b_rs])
                    xT, b_xT = xTring.next()
                    for j in range(4):
                        for half in range(2):
                            bk, b_bk = nbank()
                            for i in range(4):
                                cc = half * 4 + i
                                pe.op(lambda e: e.transpose(bk[:, i * 128:(i + 1) * 128],
                                                            xt[:, j, cc * 128:(cc + 1) * 128], ident[:]),
                                      [b_xt, b_ident], [b_bk], inc=(i == 3))
                            evac(xT[:, half * 4:(half + 1) * 4, j * 128:(j + 1) * 128],
                                 bk[:].rearrange("p (i t) -> p i t", i=4), [b_bk], [b_xT])
                    for qc in range(8):
                        bA, b_bA = nbank()
                        bB, b_bB = nbank()
                        for cc in range(8):
                            pe.op(lambda e: e.matmul(bA[:], wbf[:, cc, qc * 128:(qc + 1) * 128],
                                                     xT[:, cc, :], start=(cc == 0), stop=(cc == 7)),
                                  [b_wbf, b_xT], [b_bA], inc=(cc == 7))
                        for cc in range(8):
                            pe.op(lambda e: e.matmul(bB[:], wrot[:, cc, qc * 128:(qc + 1) * 128],
                                                     xT[:, cc, :], start=(cc == 0), stop=(cc == 7)),
                                  [b_wrot, b_xT], [b_bB], inc=(cc == 7))
                        t1, b_t1 = t1ring.next()
                        t2, b_t2 = t2ring.next()
                        qo, b_qo = qoring.next()
                        dve.op(lambda e: e.tensor_tensor(t1[:], bA[:], rc[:], ALU.mult),
                               [b_bA, b_rc], [b_t1])
                        dve.op(lambda e: e.tensor_tensor(t2[:], bB[:], rs[:], ALU.mult),
                               [b_bB, b_rs], [b_t2])
                        pool.op(lambda e: e.tensor_tensor(qo[:], t1[:], t2[:], ALU.add),
                                [b_t1, b_t2], [b_qo])
                        if qc < 4:
                            pool.dma(qT_s[qc, :, tsl], qo[:], [b_qo], [b_qT], sembuf=b_qo)
                        else:
                            pool.dma(kT_s[qc - 4, :, tsl], qo[:], [b_qo], [b_kT], sembuf=b_qo)
                    vo, b_vo = voring.next()
                    for j in range(4):
                        bk, b_bk = nbank()
                        for cc in range(8):
                            pe.op(lambda e: e.matmul(bk[:], xT[:, cc, j * 128:(j + 1) * 128],
                                                     wbf[:, cc, 1024:1536], start=(cc == 0), stop=(cc == 7)),
                                  [b_wbf, b_xT], [b_bk], inc=(cc == 7))
                        evac(vo[:, j, :], bk[:], [b_bk], [b_vo])
                    pool.dma(v_s[tsl, :].rearrange("(j p) e -> p j e", p=128), vo[:], [b_vo], [b_v],
                             sembuf=b_vo)
                    for rcn in range(8):
                        bk, b_bk = nbank()
                        for cc in range(8):
                            pe.op(lambda e: e.matmul(bk[:], wbf[:, cc, 1536 + rcn * 128:1536 + (rcn + 1) * 128],
                                                     xT[:, cc, :], start=(cc == 0), stop=(cc == 7)),
                                  [b_wbf, b_xT], [b_bk], inc=(cc == 7))
                        xo, b_xo = xgring.next()
                        evac(xo[:], bk[:], [b_bk], [b_xo])
                        pool.dma(xg_s[rcn, :, tsl], xo[:], [b_xo], [b_xg], sembuf=b_xo)
            c.barrier()

        def phase_lru(l):
            for cc in range(4):
                csl = slice(cc * 128, (cc + 1) * 128)
                with ExitStack() as es:
                    xr, b_xr = c.sb(es, "l_xr", [128, S], F32)
                    xc, b_xc = c.sb(es, "l_xc", [128, S], F32)
                    hh, b_hh = c.sb(es, "l_h", [128, S], F32)
                    rec, b_rec = c.sb(es, "l_rec", [128, S], BF16)
                    prm, b_prm = c.sb(es, "l_prm", [128, 16], F32)
                    ca, b_ca = c.sb(es, "l_ca", [128, 4], F32)
                    wbd = [[c.sb(es, "l_wbd", [128, 128], F32) for _ in range(2)] for _ in range(2)]
                    rr = c.ring(es, "l_r", [128, SBW], F32, 1)
                    ir = c.ring(es, "l_i", [128, SBW], F32, 1)
                    ar = c.ring(es, "l_a", [128, SBW], F32, 1)
                    a2r = c.ring(es, "l_a2", [128, SBW], F32, 1)
                    btr = c.ring(es, "l_bt", [128, SBW], F32, 1)
                    tmpr = c.ring(es, "l_tmp", [128, SBW], F32, 2)
                    sp.dma(xr[:], xg_s[cc, :, :], [b_xg], [b_xr])

                    def col(v):
                        return v.rearrange("(p o) -> p o", o=1)

                    cols = [conv_w[l, 0, csl], conv_w[l, 1, csl], conv_w[l, 2, csl], conv_w[l, 3, csl],
                            conv_b[l, csl], gate_a_b[l, 0, csl], gate_a_b[l, 1, csl],
                            gate_i_b[l, 0, csl], gate_i_b[l, 1, csl],
                            lru_lambda[l, 0, csl], lru_lambda[l, 1, csl]]
                    for i, v in enumerate(cols):
                        sp.dma(prm[:, i:i + 1], col(v), [], [b_prm])
                    act.op(lambda e: e.activation(ca[:, 0:2], prm[:, 9:11], AF.Exp, scale=-1.0),
                           [b_prm], [b_ca])
                    act.op(lambda e: e.activation(ca[:, 0:2], ca[:, 0:2], AF.Ln, bias=1.0),
                           [b_ca], [b_ca])
                    dve.op(lambda e: e.tensor_scalar(ca[:, 2:4], ca[:, 0:2], -16.0, None, ALU.mult),
                           [b_ca], [b_ca])
                    dve.op(lambda e: e.tensor_scalar(ca[:, 0:2], ca[:, 0:2], -8.0, None, ALU.mult),
                           [b_ca], [b_ca])
                    for d in range(2):
                        for gi, gw in enumerate((gate_a_w, gate_i_w)):
                            wt, b_wt = wbd[d][gi]
                            dve.op(lambda e: e.memset(wt[:], 0.0), [], [b_wt])
                            sp.dma(wt[0:64, 0:64], gw[l, d, 2 * cc], [], [b_wt])
                            sp.dma(wt[64:128, 64:128], gw[l, d, 2 * cc + 1], [], [b_wt])
                    dve.op(lambda e: e.tensor_scalar(xc[:], xr[:], prm[:, 2:3], prm[:, 4:5], ALU.mult, ALU.add),
                           [b_xr, b_prm], [b_xc])
                    dve.op(lambda e: e.scalar_tensor_tensor(xc[:, 2:S], xr[:, 0:S - 2], prm[:, 0:1], xc[:, 2:S],
                                                            ALU.mult, ALU.add), [b_xr, b_prm, b_xc], [b_xc])
                    dve.op(lambda e: e.scalar_tensor_tensor(xc[:, 1:S], xr[:, 0:S - 1], prm[:, 1:2], xc[:, 1:S],
                                                            ALU.mult, ALU.add), [b_xr, b_prm, b_xc], [b_xc])
                    dve.op(lambda e: e.scalar_tensor_tensor(xc[:, 0:S - 1], xr[:, 1:S], prm[:, 3:4], xc[:, 0:S - 1],
                                                            ALU.mult, ALU.add), [b_xr, b_prm, b_xc], [b_xc])
                    bi = [0]
                    for d in range(2):
                        carry = None
                        order = range(NSB) if d == 0 else range(NSB - 1, -1, -1)
                        for sbi in order:
                            ssl = slice(sbi * SBW, (sbi + 1) * SBW)
                            rt, b_rt = rr.next()
                            it, b_it = ir.next()
                            for gi, (gt, b_gt, bcol) in enumerate(((rt, b_rt, 5 + d), (it, b_it, 7 + d))):
                                wt, b_wt = wbd[d][gi]
                                for blk in range(SBW // 512):
                                    bi[0] += 1
                                    bk, b_bk = banks[bi[0] % 8]
                                    pe.op(lambda e: e.matmul(bk[:], wt[:], xc[:, sbi * SBW + blk * 512:
                                                                                sbi * SBW + (blk + 1) * 512],
                                                             start=True, stop=True), [b_wt, b_xc], [b_bk])
                                    act.op(lambda e: e.activation(gt[:, blk * 512:(blk + 1) * 512], bk[:],
                                                                  AF.Sigmoid, bias=prm[:, bcol:bcol + 1]),
                                           [b_bk, b_prm], [b_gt])
                            at, b_at = ar.next()
                            a2t, b_a2t = a2r.next()
                            bt, b_bt = btr.next()
                            act.op(lambda e: e.activation(at[:], rt[:], AF.Exp, scale=ca[:, d:d + 1]),
                                   [b_rt, b_ca], [b_at])
                            act.op(lambda e: e.activation(a2t[:], rt[:], AF.Exp, scale=ca[:, 2 + d:3 + d]),
                                   [b_rt, b_ca], [b_a2t])
                            act.op(lambda e: e.activation(a2t[:], a2t[:], AF.Sqrt, bias=1.0, scale=-1.0),
                                   [b_a2t], [b_a2t])
                            dve.op(lambda e: e.tensor_tensor(bt[:], a2t[:], it[:], ALU.mult),
                                   [b_a2t, b_it], [b_bt])
                            dve.op(lambda e: e.tensor_tensor(bt[:], bt[:], xc[:, ssl], ALU.mult),
                                   [b_bt, b_xc], [b_bt])
                            init = 0.0 if carry is None else carry[0]
                            crd = [] if carry is None else [carry[1]]
                            if d == 0:
                                dve.op(lambda e: e.tensor_tensor_scan(hh[:, ssl], at[:], bt[:], init,
                                                                      ALU.mult, ALU.add),
                                       [b_at, b_bt] + crd, [b_hh])
                                carry = (hh[:, (sbi + 1) * SBW - 1:(sbi + 1) * SBW], b_hh)
                            else:
                                tm, b_tm = tmpr.next()
                                dve.op(lambda e: e.tensor_tensor_scan(tm[:, ::-1], at[:, ::-1], bt[:, ::-1], init,
                                                                      ALU.mult, ALU.add),
                                       [b_at, b_bt] + crd, [b_tm])
                                carry = (tm[:, 0:1], b_tm)
                                pool.op(lambda e: e.tensor_tensor(hh[:, ssl], hh[:, ssl], tm[:], ALU.add),
                                        [b_hh, b_tm], [b_hh])
                    sp.dma(xr[:], xg_s[4 + cc, :, :], [b_xg], [b_xr])
                    act.op(lambda e: e.activation(xr[:], xr[:], AF.Gelu_apprx_tanh), [b_xr], [b_xr])
                    dve.op(lambda e: e.tensor_tensor(rec[:], hh[:], xr[:], ALU.mult), [b_hh, b_xr], [b_rec])
                    pool.dma(cat_s[4 + cc, :, :], rec[:], [b_rec], [b_cat], sembuf=b_rec)
                c.barrier()

        def phase_attn(l):
            lambda_init = 0.8 - 0.6 * math.exp(-0.3 * l)
            with ExitStack() as es:
                lq, b_lq = c.sb(es, "a_lq", [128, 256], F32)
                lt, b_lt = c.sb(es, "a_lt", [128, 128], F32)
                ls, b_ls = c.sb(es, "a_ls", [128, 4], F32)
                gsc, b_gsc = c.sb(es, "a_gsc", [128, 1], F32)
                sp.dma(lq[:], lambda_qk[l].partition_broadcast(128), [], [b_lq])
                dve.op(lambda e: e.tensor_tensor(lt[:, 0:64], lq[:, 0:64], lq[:, 64:128], ALU.mult),
                       [b_lq], [b_lt])
                dve.op(lambda e: e.tensor_tensor(lt[:, 64:128], lq[:, 128:192], lq[:, 192:256], ALU.mult),
                       [b_lq], [b_lt])
                dve.op(lambda e: e.tensor_reduce(ls[:, 0:2], lt[:].rearrange("p (a b) -> p a b", a=2),
                                                 AX.X, ALU.add), [b_lt], [b_ls])
                act.op(lambda e: e.activation(ls[:, 0:2], ls[:, 0:2], AF.Exp), [b_ls], [b_ls])
                dve.op(lambda e: e.tensor_scalar(ls[:, 2:3], ls[:, 1:2], -lambda_init, None, ALU.add),
                       [b_ls], [b_ls])
                dve.op(lambda e: e.tensor_tensor(ls[:, 3:4], ls[:, 2:3], ls[:, 0:1], ALU.subtract),
                       [b_ls], [b_ls])
                sp.dma(gsc[:], subln_g[l].rearrange("(p o) -> p o", o=1), [], [b_gsc])
                dve.op(lambda e: e.tensor_scalar(gsc[:], gsc[:], 1.0 - lambda_init, None, ALU.mult),
                       [b_gsc], [b_gsc])
                neglam = ls[:, 3:4]

                qT, b_qTt = c.sb(es, "a_qT", [128, S], BF16)
                kT, b_kTt = c.sb(es, "a_kT", [128, S], BF16)
                vv, b_vv = c.sb(es, "a_v", [128, NKT, 128], BF16)
                pring = c.ring(es, "a_p", [128, 512], BF16, 3)
                f1 = c.ring(es, "a_f1", [128, 512], F32, 2)
                f2 = c.ring(es, "a_f2", [128, 512], F32, 2)
                f3 = c.ring(es, "a_f3", [128, 512], F32, 2)
                f4 = c.ring(es, "a_f4", [128, 512], F32, 2)
                ores = c.ring(es, "a_o", [128, 512], BF16, 2)
                zacc = [c.ring(es, "a_z%d" % m, [128, 512], F32, 2) for m in range(2)]
                psS = [banks[0], banks[1]]
                psO = [banks[2], banks[3]]
                psZ = [banks[4], banks[5]]
                psM = banks[6]
                for h in range(4):
                    sp.dma(qT[:], qT_s[h], [b_qT], [b_qTt])
                    sp.dma(kT[:], kT_s[h], [b_kT], [b_kTt])
                    sp.dma(vv[:], v_s[:, h * 128:(h + 1) * 128].rearrange("(t p) e -> p t e", p=128),
                           [b_v], [b_vv])
                    for qb in range(NTB):
                        qsl = slice(qb * 512, (qb + 1) * 512)
                        units = [(m, kt) for m in range(2) for kt in range(NKT)]

                        def qk(i):
                            m, kt = units[i]
                            bk, b_bk = psS[i % 2]
                            pe.op(lambda e: e.matmul(bk[:], kT[64 * m:64 * m + 64, kt * 128:(kt + 1) * 128],
                                                     qT[64 * m:64 * m + 64, qsl], start=True, stop=True),
                                  [b_kTt, b_qTt], [b_bk])

                        za = [zacc[0].next(), zacc[1].next()]
                        qk(0)
                        for i, (m, kt) in enumerate(units):
                            if i + 1 < len(units):
                                qk(i + 1)
                            bk, b_bk = psS[i % 2]
                            pt, b_pt = pring.next()
                            act.op(lambda e: e.activation(pt[:], bk[:], AF.Exp, scale=0.125), [b_bk], [b_pt])
                            last = (kt == NKT - 1)
                            bo, b_bo = psO[m]
                            zt, b_zt = za[m]
                            pe.op(lambda e: e.matmul(bo[:], vv[:, kt, :], pt[:], start=(kt == 0), stop=last),
                                  [b_vv, b_pt], [b_bo], inc=True)
                            if kt == 0:
                                dve.op(lambda e: e.tensor_copy(zt[:], pt[:]), [b_pt], [b_zt])
                            else:
                                dve.op(lambda e: e.tensor_tensor(zt[:], zt[:], pt[:], ALU.add), [b_pt, b_zt], [b_zt])
                        for m in range(2):
                            bz, b_bz = psZ[m]
                            zt, b_zt = za[m]
                            pe.op(lambda e: e.matmul(bz[:], ones_f[:], zt[:], start=True, stop=True),
                                  [b_ones_f, b_zt], [b_bz])
                        r1, b_r1 = f1.next()
                        o1, b_o1 = f2.next()
                        r2, b_r2 = f3.next()
                        o2, b_o2 = f4.next()
                        act.op(lambda e: e.activation(r1[:], psZ[0][0][:], AF.Ln), [psZ[0][1]], [b_r1])
                        act.op(lambda e: e.activation(r1[:], r1[:], AF.Exp, scale=-1.0), [b_r1], [b_r1])
                        dve.op(lambda e: e.tensor_tensor(o1[:], psO[0][0][:], r1[:], ALU.mult),
                               [psO[0][1], b_r1], [b_o1])
                        act.op(lambda e: e.activation(r2[:], psZ[1][0][:], AF.Ln), [psZ[1][1]], [b_r2])
                        act.op(lambda e: e.activation(r2[:], r2[:], AF.Exp, scale=-1.0), [b_r2], [b_r2])
                        dve.op(lambda e: e.tensor_tensor(o2[:], psO[1][0][:], r2[:], ALU.mult),
                               [psO[1][1], b_r2], [b_o2])
                        dve.op(lambda e: e.scalar_tensor_tensor(o1[:], o2[:], neglam, o1[:], ALU.mult, ALU.add),
                               [b_o2, b_o1, b_ls], [b_o1])
                        pool.op(lambda e: e.tensor_tensor(r1[:], o1[:], o1[:], ALU.mult), [b_o1], [b_r1])
                        bm, b_bm = psM
                        pe.op(lambda e: e.matmul(bm[:], ones_f[:], r1[:], start=True, stop=True),
                              [b_ones_f, b_r1], [b_bm])
                        act.op(lambda e: e.activation(r2[:], bm[:], AF.Ln, bias=1e-5, scale=1.0 / 128.0),
                               [b_bm], [b_r2])
                        act.op(lambda e: e.activation(r2[:], r2[:], AF.Exp, scale=-0.5), [b_r2], [b_r2])
                        ob, b_ob = ores.next()
                        dve.op(lambda e: e.scalar_tensor_tensor(ob[:], o1[:], gsc[:, 0:1], r2[:], ALU.mult, ALU.mult),
                               [b_o1, b_gsc, b_r2], [b_ob])
                        pool.dma(cat_s[h, :, qsl], ob[:], [b_ob], [b_cat], sembuf=b_ob)
            c.barrier()

        def layer_norm(es_tiles, r, b_r, grep, b_grep, brep, b_brep):
            st, b_st, mv, b_mv = es_tiles
            for k in range(2):
                dve.op(lambda e: e.bn_stats(st[:, k, :], r[:, k * 512:(k + 1) * 512]), [b_r], [b_st])
            dve.op(lambda e: e.bn_aggr(mv[:, 0:2], st[:].rearrange("p a b -> p (a b)")), [b_st], [b_mv])
            act.op(lambda e: e.activation(mv[:, 2:3], mv[:, 1:2], AF.Ln, bias=LN_EPS), [b_mv], [b_mv])
            act.op(lambda e: e.activation(mv[:, 2:3], mv[:, 2:3], AF.Exp, scale=-0.5), [b_mv], [b_mv])
            dve.op(lambda e: e.tensor_scalar(r, r, mv[:, 0:1], mv[:, 2:3], ALU.subtract, ALU.mult),
                   [b_r, b_mv], [b_r])
            pool.op(lambda e: e.tensor_tensor(r, r, grep[:], ALU.mult), [b_r, b_grep], [b_r])
            pool.op(lambda e: e.tensor_tensor(r, r, brep[:], ALU.add), [b_r, b_brep], [b_r])

        def phase_outproj(l, xsrc, b_xsrc):
            with ExitStack() as es:
                wo, b_wo = c.sb(es, "o_w", [128, 8, D], BF16)
                g1, b_g1 = c.sb(es, "o_g", [128, D], F32)
                b1, b_b1 = c.sb(es, "o_b", [128, D], F32)
                sp.dma(g1[:], ln1_g[l].partition_broadcast(128), [], [b_g1])
                sp.dma(b1[:], ln1_b[l].partition_broadcast(128), [], [b_b1])
                wst = c.ring(es, "o_wst", [128, D], F32, 2)
                for cc in range(8):
                    st, b_st = wst.next()
                    sp.dma(st[:], w_out[l, cc * 128:(cc + 1) * 128, :], [], [b_st])
                    evac(wo[:, cc, :], st[:], [b_st], [b_wo])
                catr = c.ring(es, "o_cat", [128, 8, 512], BF16, 2)
                xr_ = c.ring(es, "o_x", [128, 4, D], F32, 2)
                rr_ = c.ring(es, "o_r", [128, D], F32, 3)
                stt = c.ring(es, "o_st", [128, 2, 6], F32, 2)
                mvt = c.ring(es, "o_mv", [128, 4], F32, 2)
                bi = [0]
                for tb in range(NTB):
                    tsl = slice(tb * 512, (tb + 1) * 512)
                    ct, b_ct = catr.next()
                    sp.dma(ct[:], cat_s[:, :, tsl].rearrange("c p t -> p c t"), [b_cat], [b_ct])
                    xt, b_xt = xr_.next()
                    sp.dma(xt[:], xsrc[tsl, :].rearrange("(j p) d -> p j d", p=128), [b_xsrc], [b_xt])
                    for j in range(4):
                        r, b_r = rr_.next()
                        for hf in range(2):
                            bi[0] += 1
                            bk, b_bk = banks[bi[0] % 8]
                            for cc in range(8):
                                pe.op(lambda e: e.matmul(bk[:], ct[:, cc, j * 128:(j + 1) * 128],
                                                         wo[:, cc, hf * 512:(hf + 1) * 512],
                                                         start=(cc == 0), stop=(cc == 7)),
                                      [b_ct, b_wo], [b_bk], inc=(cc == 7))
                            dve.op(lambda e: e.scalar_tensor_tensor(r[:, hf * 512:(hf + 1) * 512],
                                                                    xt[:, j, hf * 512:(hf + 1) * 512], ALPHA,
                                                                    bk[:], ALU.mult, ALU.add),
                                   [b_xt, b_bk], [b_r])
                        st, b_st = stt.next()
                        mv, b_mv = mvt.next()
                        layer_norm((st, b_st, mv, b_mv), r[:], b_r, g1, b_g1, b1, b_b1)
                        pool.dma(x1_s[tb * 512 + j * 128:tb * 512 + (j + 1) * 128, :], r[:], [b_r], [b_x1s],
                                 sembuf=b_r)
            c.barrier()

        def phase_peer(l, ydst, b_ydst):
            with ExitStack() as es:
                wq, b_wq = c.sb(es, "e_wq", [128, 8, 2048], BF16)
                kTt, b_kTt = c.sb(es, "e_keysT", [128, 2, 128], BF16)
                g2, b_g2 = c.sb(es, "e_g", [128, D], F32)
                b2, b_b2 = c.sb(es, "e_b", [128, D], F32)
                sp.dma(g2[:], ln2_g[l].partition_broadcast(128), [], [b_g2])
                sp.dma(b2[:], ln2_b[l].partition_broadcast(128), [], [b_b2])
                with ExitStack() as es2:
                    wst = c.ring(es2, "e_wst", [128, 2048], F32, 2)
                    for cc in range(8):
                        st, b_st = wst.next()
                        sp.dma(st[:], peer_wq[l, cc * 128:(cc + 1) * 128, :], [], [b_st])
                        evac(wq[:, cc, :], st[:], [b_st], [b_wq])
                    for p in range(2):
                        st, b_st = wst.next()
                        sp.dma(st[:, 0:128], peer_keys[l, p], [], [b_st])
                        bk, b_bk = banks[p]
                        pe.op(lambda e: e.transpose(bk[:, 0:128], st[:, 0:128], ident[:]), [b_st, b_ident], [b_bk])
                        evac(kTt[:, p, :], bk[:, 0:128], [b_bk], [b_kTt])
                    c.barrier()
                GT, b_GT = c.sb(es, "e_GT", [128, TG, 128], BF16)
                x1, b_x1 = c.sb(es, "e_x1", [128, 2, D], F32)
                x1T, b_x1T = c.sb(es, "e_x1T", [128, 8, TG], BF16)
                qpT, b_qpT = c.sb(es, "e_qpT", [128, 16, TG], BF16)
                bA, b_bA = c.sb(es, "e_bA", [128, 2048], F32)
                bB, b_bB = c.sb(es, "e_bB", [128, 2048], F32)
                bC, b_bC = c.sb(es, "e_bC", [128, 2048], F32)
                stop_, b_stop = c.sb(es, "e_stop", [128, 16, 16], F32)
                itop, b_itop = c.sb(es, "e_itop", [128, 16, 16], U32)
                itopf, b_itopf = c.sb(es, "e_itopf", [128, 16, 16], F32)
                fs, b_fs = c.sb(es, "e_fs", [128, 8, 16], F32)
                fi, b_fi = c.sb(es, "e_fi", [128, 8, 16], U32)
                fhi, b_fhi = c.sb(es, "e_fhi", [128, 8, 16], U32)
                flo, b_flo = c.sb(es, "e_flo", [128, 8, 16], U32)
                fhf, b_fhf = c.sb(es, "e_fhf", [128, 8, 16], F32)
                flf, b_flf = c.sb(es, "e_flf", [128, 8, 16], F32)
                e12g, b_e12g = c.sb(es, "e_e12g", [128, 3, 128], F32)
                sm, b_sm = c.sb(es, "e_sm", [128, 16], F32)
                eT, b_eT = c.sb(es, "e_eT", [128, 3, TG], BF16)
                Lr = c.ring(es, "e_L", [128, 16, 128], BF16, 2)
                Rr = c.ring(es, "e_R", [128, 16, 128], BF16, 2)
                eur = c.ring(es, "e_eu", [128, 1024], BF16, 4)
                evr = c.ring(es, "e_ev", [128, 1024], BF16, 4)
                glr = c.ring(es, "e_gl", [128, TG], BF16, 3)
                Hr = c.ring(es, "e_H", [128, TG], BF16, 3)
                stt = c.ring(es, "e_st", [128, 2, 6], F32, 2)
                mvt = c.ring(es, "e_mv", [128, 4], F32, 2)

                for g in range(NG):
                    t0 = g * TG
                    sp.dma(x1[:], x1_s[t0:t0 + TG, :].rearrange("(j p) d -> p j d", p=128), [b_x1s], [b_x1])
                    for j in range(2):
                        for half in range(2):
                            bk, b_bk = banks[2 + (j * 2 + half) % 2]
                            for i in range(4):
                                cc = half * 4 + i
                                pe.op(lambda e: e.transpose(bk[:, i * 128:(i + 1) * 128],
                                                            x1[:, j, cc * 128:(cc + 1) * 128], ident[:]),
                                      [b_x1, b_ident], [b_bk], inc=(i == 3))
                            evac(x1T[:, half * 4:(half + 1) * 4, j * 128:(j + 1) * 128],
                                 bk[:].rearrange("p (i t) -> p i t", i=4), [b_bk], [b_x1T])
                    for hp in range(16):
                        bk, b_bk = banks[2 + hp % 2]
                        for cc in range(8):
                            pe.op(lambda e: e.matmul(bk[:, 0:TG], wq[:, cc, hp * 128:(hp + 1) * 128], x1T[:, cc, :],
                                                     start=(cc == 0), stop=(cc == 7)),
                                  [b_wq, b_x1T], [b_bk], inc=(cc == 7))
                        evac(qpT[:, hp, :], bk[:, 0:TG], [b_bk], [b_qpT])
                    for j in range(2):
                        jsl = slice(j * 128, (j + 1) * 128)
                        for q4 in range(4):
                            bk, b_bk = banks[4 + q4]
                            for i in range(4):
                                hp = q4 * 4 + i
                                pe.op(lambda e: e.matmul(bk[:, i * 128:(i + 1) * 128], qpT[:, hp, jsl],
                                                         kTt[:, hp % 2, :], start=True, stop=True),
                                      [b_qpT, b_kTt], [b_bk], inc=(i == 3))
                            evac(bA[:, q4 * 512:(q4 + 1) * 512], bk[:], [b_bk], [b_bA])
                        for hp in range(16):
                            ksl = slice(hp * 128, (hp + 1) * 128)
                            dve.op(lambda e: e.max(stop_[:, hp, 0:8], bA[:, ksl]), [b_bA], [b_stop])
                            dve.op(lambda e: e.max_index(itop[:, hp, 0:8], stop_[:, hp, 0:8], bA[:, ksl]),
                                   [b_bA, b_stop], [b_itop])
                            dve.op(lambda e: e.match_replace(bB[:, ksl], stop_[:, hp, 0:8], bA[:, ksl], NEG),
                                   [b_bA, b_stop], [b_bB])
                            dve.op(lambda e: e.max(stop_[:, hp, 8:16], bB[:, ksl]), [b_bB], [b_stop])
                            dve.op(lambda e: e.max_index(itop[:, hp, 8:16], stop_[:, hp, 8:16], bB[:, ksl]),
                                   [b_bB, b_stop], [b_itop])
                        dve.op(lambda e: e.tensor_copy(itopf[:], itop[:]), [b_itop], [b_itopf])
                        for h in range(8):
                            in0 = stop_[:, 2 * h, :].unsqueeze(2).to_broadcast([128, 16, 16])
                            in1 = stop_[:, 2 * h + 1, :].unsqueeze(1).to_broadcast([128, 16, 16])
                            dve.op(lambda e: e.tensor_tensor(
                                bC[:, h * 256:(h + 1) * 256].rearrange("p (a b) -> p a b", b=16),
                                in0, in1, ALU.add), [b_stop], [b_bC])
                        for h in range(8):
                            ksl = slice(h * 256, (h + 1) * 256)
                            dve.op(lambda e: e.max(fs[:, h, 0:8], bC[:, ksl]), [b_bC], [b_fs])
                            dve.op(lambda e: e.max_index(fi[:, h, 0:8], fs[:, h, 0:8], bC[:, ksl]),
                                   [b_bC, b_fs], [b_fi])
                            dve.op(lambda e: e.match_replace(bB[:, ksl], fs[:, h, 0:8], bC[:, ksl], NEG),
                                   [b_bC, b_fs], [b_bB])
                            dve.op(lambda e: e.max(fs[:, h, 8:16], bB[:, ksl]), [b_bB], [b_fs])
                            dve.op(lambda e: e.max_index(fi[:, h, 8:16], fs[:, h, 8:16], bB[:, ksl]),
                                   [b_bB, b_fs], [b_fi])
                        dve.op(lambda e: e.tensor_single_scalar(fhi[:], fi[:], 4, ALU.logical_shift_right),
                               [b_fi], [b_fhi])
                        dve.op(lambda e: e.tensor_single_scalar(flo[:], fi[:], 15, ALU.bitwise_and),
                               [b_fi], [b_flo])
                        dve.op(lambda e: e.tensor_copy(fhf[:], fhi[:]), [b_fhi], [b_fhf])
                        dve.op(lambda e: e.tensor_copy(flf[:], flo[:]), [b_flo], [b_flf])
                        for which, (src, b_src) in enumerate(((fhf, b_fhf), (flf, b_flf))):
                            for h in range(8):
                                oh = bA[:, h * 256:(h + 1) * 256].rearrange("p (n i) -> p n i", i=16)
                                dve.op(lambda e: e.tensor_tensor(
                                    oh, src[:, h, :].unsqueeze(2).to_broadcast([128, 16, 16]),
                                    iota[:, 0:16].unsqueeze(1).to_broadcast([128, 16, 16]), ALU.is_equal),
                                    [b_src, b_iota], [b_bA])
                                dve.op(lambda e: e.tensor_tensor(
                                    oh, oh, itopf[:, 2 * h + which, :].unsqueeze(1).to_broadcast([128, 16, 16]),
                                    ALU.mult), [b_bA, b_itopf], [b_bA])
                            dve.op(lambda e: e.tensor_reduce(
                                e12g[:, which, :], bA[:].rearrange("p (n i) -> p n i", i=16), AX.X, ALU.add),
                                [b_bA], [b_e12g])
                        dve.op(lambda e: e.tensor_tensor(
                            bB[:, 0:128].rearrange("p (h n) -> p h n", n=16), fs[:],
                            fs[:, :, 0:1].to_broadcast([128, 8, 16]), ALU.subtract), [b_fs], [b_bB])
                        act.op(lambda e: e.activation(bB[:, 0:128], bB[:, 0:128], AF.Exp), [b_bB], [b_bB])
                        dve.op(lambda e: e.tensor_reduce(sm[:, 0:8], bB[:, 0:128].rearrange("p (h n) -> p h n", n=16),
                                                         AX.X, ALU.add), [b_bB], [b_sm])
                        dve.op(lambda e: e.reciprocal(sm[:, 8:16], sm[:, 0:8]), [b_sm], [b_sm])
                        dve.op(lambda e: e.tensor_tensor(
                            e12g[:, 2, :].rearrange("p (h n) -> p h n", n=16),
                            bB[:, 0:128].rearrange("p (h n) -> p h n", n=16),
                            sm[:, 8:16].unsqueeze(2).to_broadcast([128, 8, 16]), ALU.mult),
                            [b_bB, b_sm], [b_e12g])
                        bk, b_bk = banks[2 + j % 2]
                        for w3 in range(3):
                            pe.op(lambda e: e.transpose(bk[:, w3 * 128:(w3 + 1) * 128], e12g[:, w3, :], ident[:]),
                                  [b_e12g, b_ident], [b_bk], inc=(w3 == 2))
                        evac(eT[:, :, jsl], bk[:, 0:384].rearrange("p (w t) -> p w t", w=3), [b_bk], [b_eT])
                    iob = iota_b[:].unsqueeze(1).to_broadcast([128, 16, 128])
                    for c16 in range(TG // 16):
                        csl = slice(c16 * 16, (c16 + 1) * 16)
                        Lt, b_Lt = Lr.next()
                        Rt, b_Rt = Rr.next()
                        dve.op(lambda e: e.tensor_tensor(
                            Lt[:], iob, eT[:, 0, csl].unsqueeze(2).to_broadcast([128, 16, 128]), ALU.is_equal),
                            [b_iota_b, b_eT], [b_Lt])
                        pool.op(lambda e: e.tensor_tensor(
                            Lt[:], Lt[:], eT[:, 2, csl].unsqueeze(2).to_broadcast([128, 16, 128]), ALU.mult),
                            [b_Lt, b_eT], [b_Lt])
                        dve.op(lambda e: e.tensor_tensor(
                            Rt[:], iob, eT[:, 1, csl].unsqueeze(2).to_broadcast([128, 16, 128]), ALU.is_equal),
                            [b_iota_b, b_eT], [b_Rt])
                        for i in range(16):
                            t = c16 * 16 + i
                            pair = (t // 8) % 4
                            slot = t % 8
                            bk, b_bk = banks[pair * 2 + slot // 4]
                            col = (slot % 4) * 128
                            pe.op(lambda e: e.matmul(bk[:, col:col + 128], Rt[:, i, :], Lt[:, i, :],
                                                     start=True, stop=True), [b_Rt, b_Lt], [b_bk], inc=(slot == 7))
                            if slot == 7:
                                act.op(lambda e: e.copy(
                                    GT[:, t - 7:t + 1, :].rearrange("p t k -> p (t k)"),
                                    psum_all[:, pair * 2:pair * 2 + 2, :].rearrange("p b c -> p (b c)")),
                                    [banks[pair * 2][1], banks[pair * 2 + 1][1]], [b_GT])
                    psF = [[banks[4], banks[5]], [banks[6], banks[7]]]
                    eus = {}
                    evs = {}

                    def load_eu(k):
                        if k < 128:
                            eus[k] = eur.next()
                            sp.dma(eus[k][0][:], euT_s[l, k], [b_eu], [eus[k][1]])

                    def load_ev(k):
                        if k < 128:
                            evs[k] = evr.next()
                            sp.dma(evs[k][0][:], ev_s[l, k], [b_ev], [evs[k][1]])

                    def umm(k):
                        bu, b_bu = banks[k % 4]
                        eu, b_eu_t = eus[k]
                        for cc in range(8):
                            pe.op(lambda e: e.matmul(bu[:, 0:TG], eu[:, cc * 128:(cc + 1) * 128], x1T[:, cc, :],
                                                     start=(cc == 0), stop=(cc == 7)),
                                  [b_eu_t, b_x1T], [b_bu], inc=(cc == 7))

                    for k in range(3):
                        load_eu(k)
                        load_ev(k)
                    umm(0)
                    for k1 in range(128):
                        load_eu(k1 + 3)
                        load_ev(k1 + 3)
                        if k1 + 1 < 128:
                            umm(k1 + 1)
                        bu, b_bu = banks[k1 % 4]
                        ev, b_ev_t = evs[k1]
                        gl, b_gl = glr.next()
                        act.op(lambda e: e.activation(gl[:], bu[:, 0:TG], AF.Gelu_apprx_tanh), [b_bu], [b_gl])
                        Ht, b_Ht = Hr.next()
                        dve.op(lambda e: e.tensor_tensor(Ht[:], gl[:], GT[:, :, k1], ALU.mult),
                               [b_gl, b_GT], [b_Ht])
                        for j in range(2):
                            for hf in range(2):
                                bf, b_bf = psF[j][hf]
                                pe.op(lambda e: e.matmul(bf[:], Ht[:, j * 128:(j + 1) * 128],
                                                         ev[:, hf * 512:(hf + 1) * 512],
                                                         start=(k1 == 0), stop=(k1 == 127)),
                                      [b_Ht, b_ev_t], [b_bf], inc=(j == 1 and hf == 1))
                    for j in range(2):
                        for hf in range(2):
                            bf, b_bf = psF[j][hf]
                            dve.op(lambda e: e.scalar_tensor_tensor(x1[:, j, hf * 512:(hf + 1) * 512],
                                                                    x1[:, j, hf * 512:(hf + 1) * 512], ALPHA,
                                                                    bf[:], ALU.mult, ALU.add),
                                   [b_x1, b_bf], [b_x1])
                        st, b_st = stt.next()
                        mv, b_mv = mvt.next()
                        layer_norm((st, b_st, mv, b_mv), x1[:, j, :], b_x1, g2, b_g2, b2, b_b2)
                    pool.dma(ydst[t0:t0 + TG, :].rearrange("(j p) d -> p j d", p=128), x1[:], [b_x1], [b_ydst],
                             sembuf=b_x1)
            c.barrier()

        def on(p):
            return phases is None or p in phases

        if on("tables"):
            phase_tables()
        for slot in range(NSLOT):
            for l in range(DEPTH):
                if l == 0:
                    xsrc, b_xsrc = x_in[slot], Buf("xin")
                else:
                    xsrc, b_xsrc = xmid, b_xmid
                if l == DEPTH - 1:
                    ydst, b_ydst = y_out[slot], b_y
                else:
                    ydst, b_ydst = xmid, b_xmid
                if on("proj"):
                    phase_proj(l, xsrc, b_xsrc)
                if on("lru"):
                    phase_lru(l)
                if on("attn"):
                    phase_attn(l)
                if on("outproj"):
                    phase_outproj(l, xsrc, b_xsrc)
                if on("peer"):
                    phase_peer(l, ydst, b_ydst)
        c.barrier()
    return nc


def _consts(S):
    ident = np.eye(128, dtype=np.float32)
    iota = np.tile(np.arange(128, dtype=np.float32)[None, :], (128, 1))
    inv = (np.float32(500000.0) ** (-np.arange(0, 16, 2, dtype=np.float32) / np.float32(16))).astype(np.float32)
    ang = (np.arange(S, dtype=np.float32)[:, None] * inv[None, :]).astype(np.float32)
    cos = np.cos(ang).astype(np.float32).T
    sin = np.sin(ang).astype(np.float32).T
    rotc = np.ones((128, S), np.float32)
    rots = np.zeros((128, S), np.float32)
    for blk in range(2):
        o = blk * 64
        rotc[o:o + 8] = cos
        rotc[o + 8:o + 16] = cos
        rots[o:o + 8] = sin
        rots[o + 8:o + 16] = sin
    return {"c_ident": ident, "c_iota": iota, "c_rotc": rotc, "c_rots": rots}


_W_KEYS = ["w_in", "lambda_qk", "subln_g", "conv_w", "conv_b", "gate_a_w", "gate_a_b", "gate_i_w", "gate_i_b",
           "lru_lambda", "w_out", "ln1_g", "ln1_b", "peer_wq", "peer_keys", "expert_u", "expert_v", "ln2_g", "ln2_b"]


def _weights_map(inputs, depth):
    m = {}
    for k in _W_KEYS:
        a = np.ascontiguousarray(np.asarray(inputs[k], dtype=np.float32))
        if k == "lambda_qk":
            a = a.reshape(depth, 256)
        m[k] = a
    return m


def kernel(**inputs):
    xp = np.asarray(inputs["x_prompt"], dtype=np.float32)
    xs = np.asarray(inputs["x_sample"], dtype=np.float32)
    S = xp.shape[1]
    depth = inputs["w_in"].shape[0]
    seqs = [xp[i] for i in range(xp.shape[0])] + [xs[i] for i in range(xs.shape[0])]
    n_cores = 8
    nslot = (len(seqs) + n_cores - 1) // n_cores
    nc = build_nc(S=S, NSLOT=nslot, DEPTH=depth)
    wm = _weights_map(inputs, depth)
    cm = _consts(S)
    in_maps = []
    assign = []
    for core in range(n_cores):
        ids = []
        for s in range(nslot):
            i = core + s * n_cores
            ids.append(i if i < len(seqs) else core)
        assign.append(ids)
        xcore = np.ascontiguousarray(np.stack([seqs[i] for i in ids], axis=0))
        mp = {"x": xcore}
        mp.update(wm)
        mp.update(cm)
        in_maps.append(mp)
    res = run_bass_kernel_spmd(nc, in_maps, core_ids=list(range(n_cores)))
    outs = [None] * len(seqs)
    for core in range(n_cores):
        y = res.results[core]["y"]
        for s in range(nslot):
            i = core + s * n_cores
            if i < len(seqs):
                outs[i] = np.asarray(y[s], dtype=np.float32)
    y_prompt = np.stack(outs[:xp.shape[0]], axis=0)
    y_sample = np.stack(outs[xp.shape[0]:], axis=0)
    return (y_prompt, y_sample)
```
